# Optimizing a Trainium2 kernel written in Bass

```python
import math
import jax, jax.numpy as jnp
from jax import lax
import numpy as np

D_MODEL = 1024
BATCH = 8
SEQ = 4096
DEPTH = 2

HEAD_DIM = 64
MOBA_HEADS = 4
DIFF_HEADS = 4
DSA_HEADS = 4
DIFF_V_DIM = 2 * HEAD_DIM
IDX_HEADS = 8
IDX_DIM = 64
MOBA_BLOCK = 256
MOBA_TOPK = 3
DSA_TOPK_MAX = 256
ROPE_THETA = 10000.0
EPS = 1e-6
Q_BLOCK = 128
MOBA_Q_CHUNK = 32
D_FF = -(-8 * D_MODEL // (3 * 256)) * 256

MOBA_W = MOBA_HEADS * HEAD_DIM
DIFF_QK_W = DIFF_HEADS * 2 * HEAD_DIM
DIFF_V_W = DIFF_HEADS * DIFF_V_DIM
DSA_W = DSA_HEADS * HEAD_DIM
IDX_Q_W = IDX_HEADS * IDX_DIM
MIX_W = MOBA_W + DIFF_V_W + DSA_W
IN_COLS = 3 * MOBA_W + 2 * DIFF_QK_W + DIFF_V_W + 3 * DSA_W + IDX_Q_W + IDX_DIM + IDX_HEADS

kernel_name = "hybrid_moba_diff_dsa_block"


def rms_norm(x, g):
    xf = x.astype(jnp.float32)
    y = xf * lax.rsqrt(jnp.mean(xf * xf, axis=-1, keepdims=True) + EPS)
    return (y * g.astype(jnp.float32)).astype(x.dtype)


def rope_tables(seq, dim):
    inv = 1.0 / (ROPE_THETA ** (jnp.arange(0, dim, 2, dtype=jnp.float32) / dim))
    ang = jnp.arange(seq, dtype=jnp.float32)[:, None] * inv[None, :]
    return jnp.cos(ang), jnp.sin(ang)


def apply_rope(x, cos, sin):
    x1, x2 = jnp.split(x, 2, axis=-1)
    c = cos.astype(x.dtype)
    s = sin.astype(x.dtype)
    return jnp.concatenate([x1 * c - x2 * s, x1 * s + x2 * c], axis=-1)


def split_heads(t, n_heads):
    b, s, _ = t.shape
    return t.reshape(b, s, n_heads, -1).transpose(0, 2, 1, 3)


def merge_heads(t):
    b, h, s, d = t.shape
    return t.transpose(0, 2, 1, 3).reshape(b, s, h * d)


def moba_attention(q, k, v):
    B, H, S, D = q.shape
    nb = -(-S // MOBA_BLOCK)
    pad = nb * MOBA_BLOCK - S
    kb = jnp.pad(k, ((0, 0), (0, 0), (0, pad), (0, 0))).reshape(B, H, nb, MOBA_BLOCK, D)
    vb = jnp.pad(v, ((0, 0), (0, 0), (0, pad), (0, 0))).reshape(B, H, nb, MOBA_BLOCK, D)
    k_mean = jnp.mean(kb.astype(jnp.float32), axis=3).astype(k.dtype)
    n_sel = min(MOBA_TOPK, nb - 1)
    scale = D ** -0.5
    blk_ids = jnp.arange(nb)
    own_off = jnp.arange(MOBA_BLOCK)
    gather = jax.vmap(jax.vmap(lambda t, i: t[i]))

    def chunk(i):
        start = i * MOBA_Q_CHUNK
        qc = lax.dynamic_slice_in_dim(q, start, MOBA_Q_CHUNK, axis=2)
        qpos = start + jnp.arange(MOBA_Q_CHUNK)
        cur = start // MOBA_BLOCK
        k_own = lax.dynamic_index_in_dim(kb, cur, axis=2, keepdims=False)
        v_own = lax.dynamic_index_in_dim(vb, cur, axis=2, keepdims=False)
        own_pos = cur * MOBA_BLOCK + own_off
        s_own = jnp.einsum('bhcd,bhkd->bhck', qc, k_own).astype(jnp.float32) * scale
        s_own = jnp.where(own_pos[None, :] <= qpos[:, None], s_own, -jnp.inf)
        if n_sel == 0:
            p_own = jax.nn.softmax(s_own, axis=-1).astype(v.dtype)
            return jnp.einsum('bhck,bhkd->bhcd', p_own, v_own)
        gate = jnp.einsum('bhcd,bhnd->bhcn', qc, k_mean).astype(jnp.float32)
        gate = jnp.where(blk_ids < cur, gate, -jnp.inf)
        _, sel = lax.top_k(gate, n_sel)
        sel_ok = jnp.arange(n_sel) < cur
        k_sel = gather(kb, sel)
        v_sel = gather(vb, sel)
        s_sel = jnp.einsum('bhcd,bhcnkd->bhcnk', qc, k_sel).astype(jnp.float32) * scale
        s_sel = jnp.where(sel_ok[:, None], s_sel, -jnp.inf).reshape(B, H, MOBA_Q_CHUNK, n_sel * MOBA_BLOCK)
        p = jax.nn.softmax(jnp.concatenate([s_sel, s_own], axis=-1), axis=-1).astype(v.dtype)
        p_sel = p[..., :n_sel * MOBA_BLOCK].reshape(B, H, MOBA_Q_CHUNK, n_sel, MOBA_BLOCK)
        p_own = p[..., n_sel * MOBA_BLOCK:]
        return (jnp.einsum('bhcnk,bhcnkd->bhcd', p_sel, v_sel)
                + jnp.einsum('bhck,bhkd->bhcd', p_own, v_own))

    out = lax.map(chunk, jnp.arange(S // MOBA_Q_CHUNK))
    return out.transpose(1, 2, 0, 3, 4).reshape(B, H, S, D)


def diff_attention(q1, q2, k1, k2, v, lam):
    B, H, S, D = q1.shape
    scale = D ** -0.5
    kpos = jnp.arange(S)

    def block(i):
        start = i * Q_BLOCK
        qpos = start + jnp.arange(Q_BLOCK)
        mask = kpos[None, :] <= qpos[:, None]

        def probs(qf, kf):
            qb = lax.dynamic_slice_in_dim(qf, start, Q_BLOCK, axis=2)
            s = jnp.einsum('bhqd,bhkd->bhqk', qb, kf).astype(jnp.float32) * scale
            return jax.nn.softmax(jnp.where(mask, s, -jnp.inf), axis=-1)

        a = probs(q1, k1) - lam * probs(q2, k2)
        return jnp.einsum('bhqk,bhkv->bhqv', a.astype(v.dtype), v)

    out = lax.map(block, jnp.arange(S // Q_BLOCK))
    return out.transpose(1, 2, 0, 3, 4).reshape(B, H, S, v.shape[-1])


def dsa_attention(q, k, v, iq, ik, iw):
    B, H, S, D = q.shape
    n_top = min(DSA_TOPK_MAX, S // 4)
    scale = D ** -0.5
    kpos = jnp.arange(S)
    gather = jax.vmap(lambda t, i: t[:, i])

    def chunk(i):
        start = i * Q_BLOCK
        qpos = start + jnp.arange(Q_BLOCK)
        iqc = lax.dynamic_slice_in_dim(iq, start, Q_BLOCK, axis=2)
        iwc = lax.dynamic_slice_in_dim(iw, start, Q_BLOCK, axis=1)
        logits = jnp.einsum('bhcd,bsd->bhcs', iqc, ik).astype(jnp.float32) * (IDX_DIM ** -0.5)
        score = jnp.einsum('bch,bhcs->bcs', iwc.astype(jnp.float32), jax.nn.relu(logits))
        score = jnp.where(kpos[None, :] <= qpos[:, None], score, -jnp.inf)
        _, sel = lax.top_k(score, n_top)
        sel_ok = sel <= qpos[None, :, None]
        qc = lax.dynamic_slice_in_dim(q, start, Q_BLOCK, axis=2)
        k_sel = gather(k, sel)
        v_sel = gather(v, sel)
        s = jnp.einsum('bhcd,bhcnd->bhcn', qc, k_sel).astype(jnp.float32) * scale
        s = jnp.where(sel_ok[:, None], s, -jnp.inf)
        p = jax.nn.softmax(s, axis=-1).astype(v.dtype)
        return jnp.einsum('bhcn,bhcnd->bhcd', p, v_sel)

    out = lax.map(chunk, jnp.arange(S // Q_BLOCK))
    return out.transpose(1, 2, 0, 3, 4).reshape(B, H, S, D)


def setup_inputs(seed: int = 0) -> dict:
    key = jax.random.key(seed)
    ks = jax.random.split(key, 12)
    f32 = jnp.float32

    def nrm(k, shape, scale):
        return jax.random.normal(k, shape, f32) * scale

    return {
        "x": nrm(ks[0], (BATCH, SEQ, D_MODEL), 1.0),
        "attn_norm_g": 1.0 + nrm(ks[1], (DEPTH, D_MODEL), 0.05),
        "w_in": nrm(ks[2], (DEPTH, D_MODEL, IN_COLS), D_MODEL ** -0.5),
        "q_norm_g": 1.0 + nrm(ks[3], (DEPTH, 3, HEAD_DIM), 0.05),
        "k_norm_g": 1.0 + nrm(ks[4], (DEPTH, 3, HEAD_DIM), 0.05),
        "diff_lambda": nrm(ks[5], (DEPTH, 4, HEAD_DIM), 0.1),
        "diff_subln_g": 1.0 + nrm(ks[6], (DEPTH, DIFF_V_DIM), 0.05),
        "w_out": nrm(ks[7], (DEPTH, MIX_W, D_MODEL), MIX_W ** -0.5),
        "ffn_norm_g": 1.0 + nrm(ks[8], (DEPTH, D_MODEL), 0.05),
        "w_gate": nrm(ks[9], (DEPTH, D_MODEL, D_FF), D_MODEL ** -0.5),
        "w_up": nrm(ks[10], (DEPTH, D_MODEL, D_FF), D_MODEL ** -0.5),
        "w_down": nrm(ks[11], (DEPTH, D_FF, D_MODEL), D_FF ** -0.5),
    }


def reference(x, attn_norm_g, w_in, q_norm_g, k_norm_g, diff_lambda, diff_subln_g,
              w_out, ffn_norm_g, w_gate, w_up, w_down):
    S = x.shape[1]
    cos, sin = rope_tables(S, HEAD_DIM)
    sizes = [MOBA_W] * 3 + [DIFF_QK_W, DIFF_QK_W, DIFF_V_W] + [DSA_W] * 3 + [IDX_Q_W, IDX_DIM, IDX_HEADS]
    offsets = []
    acc = 0
    for sz in sizes[:-1]:
        acc += sz
        offsets.append(acc)

    for l in range(DEPTH):
        h = rms_norm(x, attn_norm_g[l])
        proj = h @ w_in[l]
        (mq, mk, mv, dq, dk, dv, sq, sk, sv, iq, ik, iw) = jnp.split(proj, offsets, axis=-1)
        qg, kg = q_norm_g[l], k_norm_g[l]

        def qk(t, nh, g):
            return apply_rope(rms_norm(split_heads(t, nh), g), cos, sin)

        moba = moba_attention(qk(mq, MOBA_HEADS, qg[0]), qk(mk, MOBA_HEADS, kg[0]),
                              split_heads(mv, MOBA_HEADS))

        dq_h = qk(dq, 2 * DIFF_HEADS, qg[1])
        dk_h = qk(dk, 2 * DIFF_HEADS, kg[1])
        lam_init = 0.8 - 0.6 * math.exp(-0.3 * l)
        lp = diff_lambda[l].astype(jnp.float32)
        lam = jnp.exp(jnp.sum(lp[0] * lp[1])) - jnp.exp(jnp.sum(lp[2] * lp[3])) + lam_init
        diff = diff_attention(dq_h[:, 0::2], dq_h[:, 1::2], dk_h[:, 0::2], dk_h[:, 1::2],
                              split_heads(dv, DIFF_HEADS), lam)
        diff = rms_norm(diff, diff_subln_g[l]) * (1.0 - lam_init)

        iq_h = apply_rope(split_heads(iq, IDX_HEADS), cos, sin)
        ik_r = apply_rope(ik, cos, sin)
        dsa = dsa_attention(qk(sq, DSA_HEADS, qg[2]), qk(sk, DSA_HEADS, kg[2]),
                            split_heads(sv, DSA_HEADS), iq_h, ik_r,
                            iw * (IDX_HEADS ** -0.5))

        mix = jnp.concatenate([merge_heads(moba), merge_heads(diff), merge_heads(dsa)], axis=-1)
        x = x + mix @ w_out[l]

        h2 = rms_norm(x, ffn_norm_g[l])
        x = x + (jax.nn.silu(h2 @ w_gate[l]) * (h2 @ w_up[l])) @ w_down[l]
    return x
```

```python
import math
from contextlib import ExitStack
import numpy as np
import concourse.bass as bass
import concourse.mybir as mybir
from concourse.bass_utils import run_bass_kernel_spmd

F32 = mybir.dt.float32
BF16 = mybir.dt.bfloat16
AF = mybir.ActivationFunctionType
ALU = mybir.AluOpType
AX = mybir.AxisListType

D = 1024
HD = 64
IN_COLS = 3656
D_FF = 2816
EPS = 1e-6
NEG = -30000.0
NSL = 21
VW = 1036
QK_REGIONS = [(0, 512, 0), (768, 1024, 512), (2304, 512, 1536), (3072, 576, 2048)]
SL_MQ, SL_MK, SL_DQ, SL_DK, SL_SQ, SL_SK, SL_IQ, SL_IK = 0, 2, 4, 8, 12, 14, 16, 20


class Buf:
    __slots__ = ("name", "w", "r")

    def __init__(self, name):
        self.name = name
        self.w = {}
        self.r = {}


class Sched:
    NDS = 8

    def __init__(self, nc, ctx):
        self.nc = nc
        self.E = {"pe": nc.tensor, "act": nc.scalar, "dve": nc.vector, "pool": nc.gpsimd, "sp": nc.sync}
        self.semobj = {}
        self.owner = {}
        self.cnt = {}
        for e in ("pe", "act", "dve", "pool"):
            self.semobj[e] = ctx.enter_context(nc.semaphore("s_" + e))
            self.owner[e] = e
            self.cnt[e] = 0
        self.dcnt = {}
        for q in ("sp", "pool"):
            self.dcnt[q] = 0
            for i in range(self.NDS):
                k = "d_%s%d" % (q, i)
                self.semobj[k] = ctx.enter_context(nc.semaphore(k))
                self.owner[k] = None
        self.seen = {e: {} for e in self.E}
        self.nwait = 0
        self.nins = 0

    def _collect(self, eng, r, w, is_dma=False):
        need = {}

        def add(k, v, raw):
            if (not is_dma) and self.owner.get(k) == eng and (eng == "pe" or not raw):
                return
            if need.get(k, 0) < v:
                need[k] = v
        for b in r:
            for k, v in b.w.items():
                add(k, v, True)
        for b in w:
            for k, v in b.w.items():
                add(k, v, False)
            for k, v in b.r.items():
                add(k, v, False)
        return need

    def _wait(self, eng, need):
        sn = self.seen[eng]
        for k, v in need.items():
            if sn.get(k, 0) >= v:
                continue
            own = self.owner.get(k)
            if own is not None:
                assert v <= self.cnt[own], "wait on unmaterialised token %s %d>%d" % (k, v, self.cnt[own])
            self.E[eng].wait_ge(self.semobj[k], v)
            sn[k] = v
            self.nwait += 1

    def _mark(self, key, val, r, w):
        for b in r:
            if b.r.get(key, 0) < val:
                b.r[key] = val
        for b in w:
            b.w = {key: val}
            b.r = {}

    def op(self, eng, fn, r=(), w=(), inc=True):
        self._wait(eng, self._collect(eng, r, w))
        ins = fn()
        if inc:
            self.cnt[eng] += 1
            ins.then_inc(self.semobj[eng], 1)
            val = self.cnt[eng]
        else:
            val = self.cnt[eng] + 1
        self._mark(eng, val, r, w)
        self.nins += 1
        return ins

    def dma(self, q, out, in_, r=(), w=(), **kw):
        i = self.dcnt[q]
        self.dcnt[q] += 1
        key = "d_%s%d" % (q, i % self.NDS)
        val = 16 * (i // self.NDS + 1)
        need = self._collect(q, r, w, is_dma=True)
        if i >= self.NDS and need.get(key, 0) < val - 16:
            need[key] = val - 16
        self._wait(q, need)
        self.E[q].dma_start(out=out, in_=in_, **kw).then_inc(self.semobj[key], 16)
        self._mark(key, val, r, w)
        self.nins += 1

    def barrier(self):
        need = {}
        for e in ("pe", "act", "dve", "pool"):
            if self.cnt[e] > 0:
                need[e] = self.cnt[e]
        for q, n in self.dcnt.items():
            for i in range(max(0, n - self.NDS), n):
                need["d_%s%d" % (q, i % self.NDS)] = 16 * (i // self.NDS + 1)
        for e in ("pe", "act", "dve", "pool", "sp"):
            nd = {k: v for k, v in need.items() if k != e}
            self._wait(e, nd)

    def finish(self):
        need = {}
        for q, n in self.dcnt.items():
            for i in range(max(0, n - self.NDS), n):
                key = "d_%s%d" % (q, i % self.NDS)
                need[key] = 16 * (i // self.NDS + 1)
        self._wait("sp", need)


class Prog:
    def __init__(self, S, depth, debug=()):
        self.S = S
        self.NT = S // 128
        self.NB = S // 256
        self.depth = depth
        self.debug = debug
        self.nc = bass.Bass("TRN2", target_bir_lowering=False)
        self.ctx = ExitStack()
        self.sc = Sched(self.nc, self.ctx)
        self.bufs = {}

    def dram(self, name, shape, dt, kind="Internal"):
        t = self.nc.dram_tensor(name, list(shape), dt, kind=kind).ap()
        return t, Buf(name)

    def _uniq(self, name):
        self.uid = getattr(self, "uid", 0) + 1
        return "%s_u%d" % (name, self.uid)

    def sb(self, name, shape, dt, stack=None):
        t = (stack or self.ctx).enter_context(self.nc.sbuf_tensor(self._uniq(name), list(shape), dt))
        return t, Buf(name)

    def ps(self, name, shape, dt, stack=None):
        t = (stack or self.ctx).enter_context(self.nc.psum_tensor(self._uniq(name), list(shape), dt))
        return t, Buf(name)


def build(S=4096, depth=2, stages=("A", "B", "C", "D", "E"), debug=()):
    P = Prog(S, depth, debug)
    nc, sc = P.nc, P.sc
    NT, NB = P.NT, P.NB
    op, dma = sc.op, sc.dma
    V, A, G, T = nc.vector, nc.scalar, nc.gpsimd, nc.tensor

    ext = {}

    def ein(name, shape):
        ext[name] = P.dram(name, shape, F32, kind="ExternalInput")
        return ext[name]
    x_d, x_b = ein("x", [S, D])
    ein("attn_norm_g", [depth, D])
    ein("w_in", [depth, D, IN_COLS])
    ein("q_norm_g", [depth, 3, HD])
    ein("k_norm_g", [depth, 3, HD])
    ein("diff_lambda", [depth, 4, HD])
    ein("diff_subln_g", [depth, 128])
    ein("w_out", [depth, D, D])
    ein("ffn_norm_g", [depth, D])
    ein("w_gate", [depth, D, D_FF])
    ein("w_up", [depth, D, D_FF])
    ein("w_down", [depth, D_FF, D])
    ein("c_cos", [S, 32])
    ein("c_sin", [S, 32])
    ein("c_ident", [128, 128])
    ein("c_tri", [128, 128])
    ein("c_triT", [128, 128])
    out_d, out_b = P.dram("out", [S, D], F32, kind="ExternalOutput")
    qkT_d, _ = P.dram("qkT", [NSL, 128, S], BF16)
    qkT_b = [Buf("qkT%d" % i) for i in range(NSL)]
    mq32_d, mq32_b = P.dram("mqT32", [2, 128, S], F32)
    vtm_d, vtm_b = P.dram("vtm", [S, VW], BF16)
    mix_d, mix_b = P.dram("mix", [S, D], BF16)
    xmid_d, xmid_b = P.dram("xmid", [S, D], F32)
    nmask_d, _ = P.dram("nmask", [NT, 128, S], BF16)
    nmask_b = [Buf("nmask%d" % i) for i in range(NT)]
    dbg = {}

    ident_f, ident_f_b = P.sb("ident_f", [128, 128], F32)
    ident_h, ident_h_b = P.sb("ident_h", [128, 128], BF16)
    tri_f, tri_f_b = P.sb("tri_f", [128, 128], F32)
    tri_h, tri_h_b = P.sb("tri_h", [128, 128], BF16)
    ones_f, ones_f_b = P.sb("ones_f", [128, 1], F32)
    kmean, kmean_b = P.sb("kmean", [128, 2, NB], F32)
    wabs, wabs_b = P.sb("wabs", [128, NT, 8], F32)
    wsgn, wsgn_b = P.sb("wsgn", [128, NT, 8], F32)

    dma("sp", ident_f[:], ext["c_ident"][0][:, :], r=[ext["c_ident"][1]], w=[ident_f_b])
    dma("sp", tri_f[:], ext["c_tri"][0][:, :], r=[ext["c_tri"][1]], w=[tri_f_b])
    op("dve", lambda: V.tensor_copy(out=ident_h[:], in_=ident_f[:]), r=[ident_f_b], w=[ident_h_b])
    op("dve", lambda: V.tensor_copy(out=tri_h[:], in_=tri_f[:]), r=[tri_f_b], w=[tri_h_b])
    op("dve", lambda: V.memset(ones_f[:], 1.0), w=[ones_f_b])

    triT_f, triT_f_b = P.sb("triT_f", [128, 128], F32)
    dma("sp", triT_f[:], ext["c_triT"][0][:, :], r=[ext["c_triT"][1]], w=[triT_f_b])
    C = dict(ext=ext, qkT_d=qkT_d, qkT_b=qkT_b, mq32_d=mq32_d, mq32_b=mq32_b, vtm_d=vtm_d, vtm_b=vtm_b,
             mix_d=mix_d, mix_b=mix_b, ident_h=ident_h, ident_h_b=ident_h_b, tri_h=tri_h, tri_h_b=tri_h_b,
             kmean=kmean, kmean_b=kmean_b, wabs=wabs, wabs_b=wabs_b, wsgn=wsgn, wsgn_b=wsgn_b,
             triT_f=triT_f, triT_f_b=triT_f_b, nmask_d=nmask_d, nmask_b=nmask_b)
    for l in range(depth):
        xin_d, xin_b = (x_d, x_b) if l == 0 else (xmid_d, xmid_b)
        xout_d, xout_b = (out_d, out_b) if l == depth - 1 else (xmid_d, xmid_b)
        if "A" in stages:
            phase_A(P, l, ext, xin_d, xin_b, qkT_d, qkT_b, mq32_d, mq32_b, vtm_d, vtm_b,
                    ident_f, ident_f_b, ident_h, ident_h_b, ones_f, ones_f_b,
                    kmean, kmean_b, wabs, wabs_b, wsgn, wsgn_b)
        sc.barrier()
        if "B" in stages:
            phase_moba(P, l, C)
            sc.barrier()
        if "C" in stages:
            phase_diff(P, l, C)
            sc.barrier()
        if "D" in stages:
            phase_dsa(P, l, C)
            sc.barrier()
        if "E" in stages:
            phase_ffn(P, l, C, xin_d, xin_b, xout_d, xout_b)
            sc.barrier()
    if "M" in debug:
        with ExitStack() as st_:
            tmpm, tmpm_b = P.sb("dbgm", [128, D], BF16, st_)
            dm, dmb = P.dram("dbg_mix", [S, D], BF16, kind="ExternalOutput")
            for t in range(NT):
                dma("sp", tmpm[:], mix_d[t * 128:(t + 1) * 128, :], r=[mix_b], w=[tmpm_b])
                dma("sp", dm[t * 128:(t + 1) * 128, :], tmpm[:], r=[tmpm_b], w=[dmb])

    if "A" in debug:
        for nm, src, sbuf_, shape in (("kmean", kmean, kmean_b, [128, 2 * NB]),
                                      ("wabs", wabs, wabs_b, [128, NT * 8]),
                                      ("wsgn", wsgn, wsgn_b, [128, NT * 8])):
            d_, b_ = P.dram("dbg_" + nm, shape, F32, kind="ExternalOutput")
            flat = src[:].rearrange("p a b -> p (a b)")
            dma("sp", d_[:, :], flat, r=[sbuf_], w=[b_])
        with ExitStack() as st:
            tmp, tmp_b = P.sb("dbgtmp", [128, S], BF16, st)
            d_, b_ = P.dram("dbg_qkT", [NSL, 128, S], BF16, kind="ExternalOutput")
            for i in range(NSL):
                dma("sp", tmp[:], qkT_d[i, :, :], r=[qkT_b[i]], w=[tmp_b])
                dma("sp", d_[i, :, :], tmp[:], r=[tmp_b], w=[b_])
            tmp2, tmp2_b = P.sb("dbgtmp2", [128, VW], BF16, st)
            d2, b2 = P.dram("dbg_vtm", [S, VW], BF16, kind="ExternalOutput")
            for t in range(NT):
                dma("sp", tmp2[:], vtm_d[t * 128:(t + 1) * 128, :], r=[vtm_b], w=[tmp2_b])
                dma("sp", d2[t * 128:(t + 1) * 128, :], tmp2[:], r=[tmp2_b], w=[b2])
            tmp3, tmp3_b = P.sb("dbgtmp3", [128, S], F32, st)
            d3, b3 = P.dram("dbg_mq32", [2, 128, S], F32, kind="ExternalOutput")
            for i in range(2):
                dma("sp", tmp3[:], mq32_d[i, :, :], r=[mq32_b], w=[tmp3_b])
                dma("sp", d3[i, :, :], tmp3[:], r=[tmp3_b], w=[b3])
    sc.finish()
    P.ctx.close()
    return P


def load_weight(P, st, name, src_d, src_b, rows, cols, g_tile, g_b, engines=("dve", "pool", "act")):
    nc, sc = P.nc, P.sc
    KC = rows // 128
    wb, wb_b = P.sb(name, [128, KC, cols], BF16, st)
    CH = 1024
    with ExitStack() as st2:
        stg = [P.sb("%s_stg%d" % (name, i), [128, CH], F32, st2) for i in range(3)]
        n = 0
        for kc in range(KC):
            for c0 in range(0, cols, CH):
                cw = min(CH, cols - c0)
                s_t, s_b = stg[n % 3]
                sc.dma("sp", s_t[:, :cw], src_d[kc * 128:(kc + 1) * 128, c0:c0 + cw],
                       r=[src_b], w=[s_b])
                e = engines[n % len(engines)]
                dst = wb[:, kc, c0:c0 + cw]
                if g_tile is None:
                    if e == "act":
                        sc.op(e, lambda: nc.scalar.copy(out=dst, in_=s_t[:, :cw]), r=[s_b], w=[wb_b])
                    else:
                        sc.op(e, lambda: sc.E[e].tensor_copy(out=dst, in_=s_t[:, :cw]), r=[s_b], w=[wb_b])
                else:
                    gs = g_tile[:, kc:kc + 1]
                    if e == "act":
                        sc.op(e, lambda: nc.scalar.activation(out=dst, in_=s_t[:, :cw], func=AF.Copy, scale=gs),
                              r=[s_b, g_b], w=[wb_b])
                    else:
                        sc.op(e, lambda: sc.E[e].tensor_scalar(out=dst, in0=s_t[:, :cw], scalar1=gs, scalar2=None,
                                                               op0=ALU.mult), r=[s_b, g_b], w=[wb_b])
                n += 1
        sc.barrier()
    return wb, wb_b


def phase_A(P, l, ext, xin_d, xin_b, qkT_d, qkT_b, mq32_d, mq32_b, vtm_d, vtm_b,
            ident_f, ident_f_b, ident_h, ident_h_b, ones_f, ones_f_b,
            kmean, kmean_b, wabs, wabs_b, wsgn, wsgn_b):
    nc, sc = P.nc, P.sc
    S, NT = P.S, P.NT
    op, dma = sc.op, sc.dma
    V, A, G, T = nc.vector, nc.scalar, nc.gpsimd, nc.tensor
    TG = min(4, NT)
    with ExitStack() as st:
        g_at, g_at_b = P.sb("g_at", [128, 8], F32, st)
        gsrc = ext["attn_norm_g"][0]
        dma("sp", g_at[:], bass.AP(tensor=gsrc.tensor, offset=l * D, ap=[[1, 128], [128, 8]]),
            r=[ext["attn_norm_g"][1]], w=[g_at_b], allow_slow_non_contiguous=True)
        gqk, gqk_b = P.sb("gqk", [128, 32, HD], F32, st)
        for (nm, gi, h0, nh) in (("q_norm_g", 0, 0, 4), ("k_norm_g", 0, 4, 4), ("q_norm_g", 1, 8, 8),
                                 ("k_norm_g", 1, 16, 8), ("q_norm_g", 2, 24, 4), ("k_norm_g", 2, 28, 4)):
            src = ext[nm][0]
            dma("sp", gqk[:, h0:h0 + nh, :],
                bass.AP(tensor=src.tensor, offset=(l * 3 + gi) * HD, ap=[[0, 128], [0, nh], [1, HD]]),
                r=[ext[nm][1]], w=[gqk_b])
        wsrc = ext["w_in"][0]
        wb, wb_b = load_weight(P, st, "w_in_b", wsrc[l], ext["w_in"][1], D, IN_COLS, g_at, g_at_b)

        if "W" in P.debug:
            d_, b_ = P.dram("dbg_wb", [128, 8, IN_COLS], BF16, kind="ExternalOutput")
            dma("sp", d_[:, :, :], wb[:], r=[wb_b], w=[b_])
        xt = [P.sb("xt%d" % i, [128, D], F32, st) for i in range(3)]
        junk, junk_b = P.sb("junkA", [128, D], BF16, st)
        ssum, ssum_b = P.sb("ssum", [128, 1], F32, st)
        rstd, rstd_b = P.sb("rstd", [128, 1], F32, st)
        hbs = [P.sb("hb%d" % i, [128, D], BF16, st) for i in range(2)]
        hTs = [P.sb("hT%d" % i, [128, 8, 128], BF16, st) for i in range(2)]
        pjs = [P.sb("pj%d" % i, [128, IN_COLS], F32, st) for i in range(2)]
        sq, sq_b = P.sb("sq", [128, 2048], F32, st)
        ss, ss_b = P.sb("ss", [128, 32], F32, st)
        rinv, rinv_b = P.sb("rinv", [128, 32], F32, st)
        nrm, nrm_b = P.sb("nrm", [128, 2624], F32, st)
        rp, rp_b = P.sb("rp", [128, 2624], F32, st)
        t1, t1_b = P.sb("t1", [128, 41 * 32], F32, st)
        t2, t2_b = P.sb("t2", [128, 41 * 32], F32, st)
        t3, t3_b = P.sb("t3", [128, 41 * 32], F32, st)
        t4, t4_b = P.sb("t4", [128, 41 * 32], F32, st)
        css = [P.sb("cs%d" % i, [128, 32], F32, st) for i in range(3)]
        sns = [P.sb("sn%d" % i, [128, 32], F32, st) for i in range(3)]
        qk_tm, qk_tm_b = P.sb("qk_tm", [128, NSL * 128], BF16, st)
        qkst = [P.sb("qkst%d" % i, [128, NSL, TG * 128], BF16, st) for i in range(1)]
        mqst = [P.sb("mqst%d" % i, [128, 2, TG * 128], F32, st) for i in range(1)]
        vt = [P.sb("vt%d" % i, [128, VW], BF16, st) for i in range(2)]
        kps_s, kps_s_b = P.sb("kps_s", [128, 2], F32, st)
        wtmp, wtmp_b = P.sb("wtmp", [128, 8], F32, st)
        pp = [P.ps("ppA%d" % i, [128, 512], F32, st) for i in range(3)]
        ptr = [P.ps("ptrA%d" % i, [128, 1024], BF16, st) for i in range(2)]
        ptf, ptf_b = P.ps("ptfA", [128, 512], F32, st)
        kps, kps_b = P.ps("kpsA", [128, 512], F32, st)

        for i in range(2):
            op("dve", lambda: V.memset(vt[i][0][:], 1.0), w=[vt[i][1]])

        state = {"npp": 0, "nptr": 0}

        pend = []

        def tick():
            if pend:
                pend.pop(0)()

        def load(t):
            x_t, x_tb = xt[t % 3]
            cs, cs_b = css[t % 3]
            sn, sn_b = sns[t % 3]
            rows = slice(t * 128, (t + 1) * 128)
            dma("sp", x_t[:], xin_d[rows, :], r=[xin_b], w=[x_tb])
            dma("sp", cs[:], ext["c_cos"][0][rows, :], r=[ext["c_cos"][1]], w=[cs_b])
            dma("sp", sn[:], ext["c_sin"][0][rows, :], r=[ext["c_sin"][1]], w=[sn_b])

        def front(t):
            npp = state["npp"]
            nptr = state["nptr"]
            hb, hb_b = hbs[t % 2]
            hT, hT_b = hTs[t % 2]
            pj, pj_b = pjs[t % 2]
            cs, cs_b = css[t % 3]
            sn, sn_b = sns[t % 3]
            x_t, x_tb = xt[t % 3]
            rows = slice(t * 128, (t + 1) * 128)
            op("act", lambda: A.activation(out=junk[:], in_=x_t[:], func=AF.Square, accum_out=ssum[:]),
               r=[x_tb], w=[junk_b, ssum_b])
            op("act", lambda: A.activation(out=rstd[:], in_=ssum[:], func=AF.Sqrt, scale=1.0 / D, bias=EPS),
               r=[ssum_b], w=[rstd_b])
            op("dve", lambda: V.reciprocal(out=rstd[:], in_=rstd[:]), r=[rstd_b], w=[rstd_b])
            op("dve", lambda: V.tensor_scalar(out=hb[:], in0=x_t[:], scalar1=rstd[:, 0:1], scalar2=None,
                                               op0=ALU.mult), r=[x_tb, rstd_b], w=[hb_b])
            pt_t, pt_b = ptr[nptr % 2]
            nptr += 1
            for kc in range(8):
                op("pe", lambda: T.transpose(out=pt_t[:, kc * 128:(kc + 1) * 128], in_=hb[:, kc * 128:(kc + 1) * 128],
                                             identity=ident_h[:]), r=[hb_b, ident_h_b], w=[pt_b], inc=(kc == 7))
            op("act", lambda: A.copy(out=hT[:].rearrange("p a b -> p (a b)"), in_=pt_t[:]), r=[pt_b], w=[hT_b])
            chunks = list(enumerate(range(0, IN_COLS, 512)))
            tiles = {}

            def mm(ci, c0):
                cw = min(512, IN_COLS - c0)
                p_t, p_b = pp[state["npp"] % 3]
                state["npp"] += 1
                tiles[ci] = (p_t, p_b)
                for kc in range(8):
                    op("pe", lambda: T.matmul(out=p_t[:, :cw], lhsT=hT[:, kc, :], rhs=wb[:, kc, c0:c0 + cw],
                                              start=(kc == 0), stop=(kc == 7)),
                       r=[hT_b, wb_b], w=[p_b], inc=(kc == 7))

            def make_evac(ci, c0):
                def ev():
                    cw = min(512, IN_COLS - c0)
                    p_t, p_b = tiles[ci]
                    if ci % 2 == 0:
                        op("act", lambda: A.copy(out=pj[:, c0:c0 + cw], in_=p_t[:, :cw]), r=[p_b], w=[pj_b])
                    else:
                        op("dve", lambda: V.tensor_copy(out=pj[:, c0:c0 + cw], in_=p_t[:, :cw]), r=[p_b], w=[pj_b])
                    if ci + 3 < len(chunks):
                        mm(*chunks[ci + 3])
                return ev
            state["npp"] = npp
            for ci, c0 in chunks[:3]:
                mm(ci, c0)
            npp = state["npp"]
            for ci, c0 in chunks:
                pend.append(make_evac(ci, c0))
            state["nptr"] = nptr
            return
            state["npp"] = npp
            state["nptr"] = nptr

        def back(t):
            nptr = state["nptr"]
            rows = slice(t * 128, (t + 1) * 128)
            pj, pj_b = pjs[t % 2]
            cs, cs_b = css[t % 3]
            sn, sn_b = sns[t % 3]
            for (s0, n, d0) in QK_REGIONS[:3]:
                op("act", lambda: A.activation(out=sq[:, d0:d0 + n], in_=pj[:, s0:s0 + n], func=AF.Square),
                   r=[pj_b], w=[sq_b])
            op("dve", lambda: V.tensor_reduce(out=ss[:], in_=sq[:].rearrange("p (h d) -> p h d", d=HD),
                                              axis=AX.X, op=ALU.add), r=[sq_b], w=[ss_b])
            op("act", lambda: A.activation(out=rinv[:], in_=ss[:], func=AF.Sqrt, scale=1.0 / HD, bias=EPS),
               r=[ss_b], w=[rinv_b])
            op("dve", lambda: V.reciprocal(out=rinv[:], in_=rinv[:]), r=[rinv_b], w=[rinv_b])
            tick()
            for (s0, n, d0) in QK_REGIONS[:3]:
                nh = n // HD
                h0 = d0 // HD
                op("dve", lambda: V.tensor_tensor(
                    out=nrm[:, d0:d0 + n].rearrange("p (h d) -> p h d", d=HD),
                    in0=pj[:, s0:s0 + n].rearrange("p (h d) -> p h d", d=HD),
                    in1=rinv[:, h0:h0 + nh].unsqueeze(2).to_broadcast([128, nh, HD]), op=ALU.mult),
                    r=[pj_b, rinv_b], w=[nrm_b])
            op("dve", lambda: V.tensor_tensor(out=nrm[:, 0:2048], in0=nrm[:, 0:2048],
                                              in1=gqk[:].rearrange("p h d -> p (h d)"), op=ALU.mult),
               r=[nrm_b, gqk_b], w=[nrm_b])
            tick()
            op("act", lambda: A.copy(out=nrm[:, 2048:2624], in_=pj[:, 3072:3648]), r=[pj_b], w=[nrm_b])
            nv = nrm[:].rearrange("p (h two d) -> p h two d", two=2, d=32)
            rv = rp[:].rearrange("p (h two d) -> p h two d", two=2, d=32)
            x1, x2 = nv[:, :, 0, :], nv[:, :, 1, :]
            cb = cs[:].unsqueeze(1).to_broadcast([128, 41, 32])
            sbb = sn[:].unsqueeze(1).to_broadcast([128, 41, 32])
            tv = [tt[:].rearrange("p (h d) -> p h d", d=32) for tt in (t1, t2, t3, t4)]
            op("dve", lambda: V.tensor_tensor(out=tv[0], in0=x1, in1=cb, op=ALU.mult), r=[nrm_b, cs_b], w=[t1_b])
            tick()
            op("dve", lambda: V.tensor_tensor(out=tv[1], in0=x2, in1=sbb, op=ALU.mult), r=[nrm_b, sn_b], w=[t2_b])
            tick()
            op("dve", lambda: V.tensor_tensor(out=tv[2], in0=x1, in1=sbb, op=ALU.mult), r=[nrm_b, sn_b], w=[t3_b])
            tick()
            op("dve", lambda: V.tensor_tensor(out=tv[3], in0=x2, in1=cb, op=ALU.mult), r=[nrm_b, cs_b], w=[t4_b])
            tick()
            op("dve", lambda: V.tensor_tensor(out=rv[:, :, 0, :], in0=tv[0], in1=tv[1], op=ALU.subtract),
               r=[t1_b, t2_b], w=[rp_b])
            op("dve", lambda: V.tensor_tensor(out=rv[:, :, 1, :], in0=tv[2], in1=tv[3], op=ALU.add),
               r=[t3_b, t4_b], w=[rp_b])
            op("act", lambda: A.copy(out=qk_tm[:, 0:2624], in_=rp[:, :]), r=[rp_b], w=[qk_tm_b])
            tick()
            op("pool", lambda: G.tensor_copy(out=qk_tm[:, 2624:2688], in_=rp[:, 2560:2624]), r=[rp_b], w=[qk_tm_b])
            g = t // TG
            ti = t % TG
            qs_t, qs_b = qkst[0]
            for j0 in range(0, NSL, 8):
                nj = min(8, NSL - j0)
                pt_t, pt_b = ptr[nptr % 2]
                nptr += 1
                for j in range(nj):
                    op("pe", lambda: T.transpose(out=pt_t[:, j * 128:(j + 1) * 128],
                                                 in_=qk_tm[:, (j0 + j) * 128:(j0 + j + 1) * 128], identity=ident_h[:]),
                       r=[qk_tm_b, ident_h_b], w=[pt_b], inc=(j == nj - 1))
                src = pt_t[:, :nj * 128].rearrange("p (a b) -> p a b", b=128)
                dst = qs_t[:, j0:j0 + nj, ti * 128:(ti + 1) * 128]
                if (j0 // 8) % 2 == 0:
                    op("act", lambda: A.copy(out=dst, in_=src), r=[pt_b], w=[qs_b])
                else:
                    op("dve", lambda: V.tensor_copy(out=dst, in_=src), r=[pt_b], w=[qs_b])
            ms_t, ms_b = mqst[0]
            for j in range(2):
                op("pe", lambda: T.transpose(out=ptf[:, j * 128:(j + 1) * 128], in_=rp[:, j * 128:(j + 1) * 128],
                                             identity=ident_f[:]), r=[rp_b, ident_f_b], w=[ptf_b], inc=(j == 1))
            op("dve", lambda: V.tensor_copy(out=ms_t[:, :, ti * 128:(ti + 1) * 128],
                                            in_=ptf[:, 0:256].rearrange("p (a b) -> p a b", b=128)), r=[ptf_b], w=[ms_b])
            if ti == TG - 1:
                c0 = g * TG * 128
                for j in range(NSL):
                    dma("sp", qkT_d[j, :, c0:c0 + TG * 128], qs_t[:, j, :], r=[qs_b], w=[qkT_b[j]])
                for j in range(2):
                    dma("sp", mq32_d[j, :, c0:c0 + TG * 128], ms_t[:, j, :], r=[ms_b], w=[mq32_b])
            for j in range(2):
                op("pe", lambda: T.matmul(out=kps[:, j:j + 1], lhsT=rp[:, 256 + j * 128:256 + (j + 1) * 128],
                                          rhs=ones_f[:, 0:1], start=True, stop=True),
                   r=[rp_b, ones_f_b], w=[kps_b], inc=(j == 1))
            b = t // 2
            if t % 2 == 0:
                op("dve", lambda: V.tensor_copy(out=kmean[:, :, b], in_=kps[:, 0:2]), r=[kps_b], w=[kmean_b])
            else:
                op("dve", lambda: V.tensor_copy(out=kps_s[:], in_=kps[:, 0:2]), r=[kps_b], w=[kps_s_b])
                op("dve", lambda: V.tensor_tensor(out=kmean[:, :, b], in0=kmean[:, :, b], in1=kps_s[:], op=ALU.add),
                   r=[kps_s_b, kmean_b], w=[kmean_b])
            v_t, v_b = vt[t % 2]
            op("act", lambda: A.copy(out=v_t[:, 0:260].rearrange("p (h d) -> p h d", d=65)[:, :, 0:64],
                                     in_=pj[:, 512:768].rearrange("p (h d) -> p h d", d=64)), r=[pj_b], w=[v_b])
            op("act", lambda: A.copy(out=v_t[:, 260:776].rearrange("p (h d) -> p h d", d=129)[:, :, 0:128],
                                     in_=pj[:, 1792:2304].rearrange("p (h d) -> p h d", d=128)),
               r=[pj_b], w=[v_b])
            op("act", lambda: A.copy(out=v_t[:, 776:1036].rearrange("p (h d) -> p h d", d=65)[:, :, 0:64],
                                     in_=pj[:, 2816:3072].rearrange("p (h d) -> p h d", d=64)), r=[pj_b], w=[v_b])
            dma("sp", vtm_d[rows, :], v_t[:], r=[v_b], w=[vtm_b])
            op("dve", lambda: V.tensor_scalar(out=wtmp[:], in0=pj[:, 3648:3656], scalar1=8.0 ** -1.5, scalar2=None,
                                               op0=ALU.mult), r=[pj_b], w=[wtmp_b])
            op("dve", lambda: V.scalar_tensor_tensor(out=wabs[:, t, :], in0=wtmp[:], scalar=-1.0, in1=wtmp[:],
                                                      op0=ALU.mult, op1=ALU.max), r=[wtmp_b], w=[wabs_b])
            op("act", lambda: A.activation(out=wsgn[:, t, :], in_=wtmp[:], func=AF.Sign), r=[wtmp_b], w=[wsgn_b])
            while pend:
                tick()
            state["nptr"] = nptr

        load(0)
        if NT > 1:
            load(1)
        front(0)
        while pend:
            tick()
        for t in range(NT):
            if t + 2 < NT:
                load(t + 2)
            if t + 1 < NT:
                front(t + 1)
            back(t)


def rope_tables_np(S):
    inv = (1.0 / (np.float32(10000.0) ** (np.arange(0, HD, 2, dtype=np.float32) / np.float32(HD)))).astype(np.float32)
    ang = np.arange(S, dtype=np.float32)[:, None] * inv[None, :]
    return np.cos(ang).astype(np.float32), np.sin(ang).astype(np.float32)


def const_inputs(S):
    cos, sin = rope_tables_np(S)
    k = np.arange(128)[:, None]
    q = np.arange(128)[None, :]
    tri = np.where(k <= q, 0.0, NEG).astype(np.float32)
    triT = np.where(k.T <= q.T, 0.0, -1e30).astype(np.float32)
    return {"c_cos": cos, "c_sin": sin, "c_ident": np.eye(128, dtype=np.float32), "c_tri": tri, "c_triT": triT}


_PROG = {}


def kernel(**inputs):
    x = np.ascontiguousarray(inputs["x"], dtype=np.float32)
    B, S, _ = x.shape
    depth = inputs["w_in"].shape[0]
    key = (S, depth)
    if key not in _PROG:
        _PROG[key] = build(S, depth)
    P = _PROG[key]
    consts = const_inputs(S)
    shared = {k: np.ascontiguousarray(v, dtype=np.float32) for k, v in inputs.items() if k != "x"}
    in_maps = []
    for b in range(B):
        m = {"x": x[b]}
        m.update(shared)
        m.update(consts)
        in_maps.append(m)
    res = run_bass_kernel_spmd(P.nc, in_maps, core_ids=list(range(B)))
    return np.stack([np.asarray(r["out"]) for r in res.results], axis=0).astype(np.float32)


def load_slices(P, st, name, qkT_d, qkT_b, s0, n, dt=BF16):
    t, b = P.sb(name, [128, n, P.S], dt, st)
    for j in range(n):
        P.sc.dma("sp", t[:, j, :], qkT_d[s0 + j, :, :], r=[qkT_b[s0 + j] if isinstance(qkT_b, list) else qkT_b], w=[b])
    return t, b


def load_v(P, st, name, vtm_d, vtm_b, c0, w):
    NT = P.NT
    t, b = P.sb(name, [128, NT, w], BF16, st)
    for t0 in range(0, NT, 8):
        n = min(8, NT - t0)
        src = vtm_d[t0 * 128:(t0 + n) * 128, c0:c0 + w].rearrange("(t p) c -> p t c", p=128)
        P.sc.dma("sp", t[:, t0:t0 + n, :], src, r=[vtm_b], w=[b])
    return t, b


class AttnPipe:
    def __init__(self, P, st, tag, n_st=4):
        self.P = P
        self.n_st = n_st
        self.stp = [P.ps("st%s%d" % (tag, i), [128, 512], F32, st) for i in range(n_st)]
        self.ptp = [P.sb("pt%s%d" % (tag, i), [128, 512], BF16, st) for i in range(4)]
        self.acc = [P.ps("acc%s%d" % (tag, i), [128, 512], F32, st) for i in range(2)]
        self.n = 0
        self.nacc = 0
        self.pending = None

    def run(self, jobs):
        units = []
        for job in jobs:
            kts = job["ktiles"]
            chunks = [kts[i:i + 4] for i in range(0, len(kts), 4)]
            for ci, ch in enumerate(chunks):
                for si in range(len(job["streams"])):
                    units.append(dict(job=job, ci=ci, si=si, ch=ch, first=(ci == 0), last=(ci == len(chunks) - 1)))
        if not units:
            return
        LA = 2
        for j in range(min(LA, len(units))):
            self._qk(units[j])
        for i, u in enumerate(units):
            if i + LA < len(units):
                self._qk(units[i + LA])
            self._exp_pv(u)

    def _qk(self, u):
        P = self.P
        nc, sc = P.nc, P.sc
        T = nc.tensor
        job = u["job"]
        if u["ci"] == 0 and u["si"] == 0:
            if job.get("prep"):
                job["prep"](job)
            job["acc"] = self.acc[self.nacc % 2]
            self.nacc += 1
        s = job["streams"][u["si"]]
        qt = job["qt"]
        st_t, st_b = self.stp[self.n % self.n_st]
        u["st"] = (st_t, st_b)
        u["pt"] = self.ptp[self.n % 4]
        self.n += 1
        qT, q_b, qsl, qh = s["q"]
        kT, k_b, ksl, kh = s["k"]
        qr = slice(qh * 64, qh * 64 + 64)
        kr = slice(kh * 64, kh * 64 + 64)
        n = len(u["ch"])
        for i, kt in enumerate(u["ch"]):
            m = job["maskfn"](u["si"], kt) if job.get("maskfn") else None
            o = st_t[:, i * 128:(i + 1) * 128]
            sc.op("pe", lambda: T.matmul(out=o, lhsT=kT[kr, ksl, kt * 128:(kt + 1) * 128],
                                         rhs=qT[qr, qsl, qt * 128:(qt + 1) * 128], start=True, stop=(m is None)),
                  r=[q_b, k_b], w=[st_b], inc=(m is None and i == n - 1))
            if m is not None:
                ml, mr, mb = m
                sc.op("pe", lambda: T.matmul(out=o, lhsT=ml, rhs=mr, start=False, stop=True, skip_group_check=True),
                      r=mb, w=[st_b], inc=(i == n - 1))

    def _exp_pv(self, u):
        P = self.P
        nc, sc = P.nc, P.sc
        T, A = nc.tensor, nc.scalar
        job = u["job"]
        s = job["streams"][u["si"]]
        st_t, st_b = u["st"]
        pt_t, pt_b = u["pt"]
        acc_t, acc_b = job["acc"]
        n = len(u["ch"])
        sc.op("act", lambda: A.activation(out=pt_t[:, :n * 128], in_=st_t[:, :n * 128], func=AF.Exp, scale=0.125),
              r=[st_b], w=[pt_b])
        vT, v_b, voff, vw = s["v"]
        ao = s["aoff"]
        nstreams = len(job["streams"])
        for i, kt in enumerate(u["ch"]):
            is_first = u["first"] and i == 0
            is_last = u["last"] and i == n - 1
            sc.op("pe", lambda: T.matmul(out=acc_t[:, ao:ao + vw], lhsT=pt_t[:, i * 128:(i + 1) * 128],
                                         rhs=vT[:, kt, voff:voff + vw], start=(is_first and u["si"] == 0),
                                         stop=is_last, skip_group_check=True),
                  r=[pt_b, v_b], w=[acc_b], inc=(is_last and u["si"] == nstreams - 1))
        if u["last"] and u["si"] == nstreams - 1:
            job["fin"](job, acc_t, acc_b)


def causal_tri_mask(ident_h, ident_h_b, tri_h, tri_h_b, qt):
    def f(si, kt):
        if kt == qt:
            return (ident_h[:], tri_h[:], [ident_h_b, tri_h_b])
        return None
    return f


def phase_moba(P, l, C):
    nc, sc = P.nc, P.sc
    op, dma = sc.op, sc.dma
    V, A, G, T = nc.vector, nc.scalar, nc.gpsimd, nc.tensor
    NT, NB = P.NT, P.NB
    with ExitStack() as st:
        qT, q_b = load_slices(P, st, "mo_q", C["qkT_d"], C["qkT_b"], SL_MQ, 2)
        kT, k_b = load_slices(P, st, "mo_k", C["qkT_d"], C["qkT_b"], SL_MK, 2)
        q32, q32_b = load_slices(P, st, "mo_q32", C["mq32_d"], C["mq32_b"], 0, 2, F32)
        vT, v_b = load_v(P, st, "mo_v", C["vtm_d"], C["vtm_b"], 0, 260)
        pipe = AttnPipe(P, st, "mo", n_st=3)
        dsa_prep = make_dsa_prep(P, st, l, C)
        q0 = dsa_split(NT)
        gps, gps_b = P.ps("mo_gps", [128, 32, 16], F32, st)
        gt = [P.sb("mo_gt%d" % i, [128, 16], F32, st) for i in range(4)]
        top8, top8_b = P.sb("mo_top8", [128, 8], F32, st)
        nsel = [P.sb("mo_nsel%d" % i, [128, 2, 16], BF16, st) for i in range(2)]
        nselx = [P.sb("mo_nselx%d" % i, [128, 2, 16, 128], BF16, st) for i in range(2)]
        rcp, rcp_b = P.sb("mo_rcp", [128, 2], F32, st)
        ot = [P.sb("mo_ot%d" % i, [128, 256], BF16, st) for i in range(2)]
        for i in range(4):
            op("dve", lambda: V.memset(gt[i][0][:], -1e30), w=[gt[i][1]])
        kmean, kmean_b = C["kmean"], C["kmean_b"]
        kmz, kmz_b = P.sb("mo_kmz", [128, 2, 2, NB], F32, st)
        op("dve", lambda: V.memset(kmz[:], 0.0), w=[kmz_b])
        for hh in range(2):
            rows = slice(hh * 64, hh * 64 + 64)
            op("dve", lambda: V.tensor_copy(out=kmz[rows, :, hh, :], in_=kmean[rows, :, :]), r=[kmean_b], w=[kmz_b])
        state = {"n": 0}

        def prep(job):
            qt, ss = job["qt"], job["ss"]
            qp = qt - (NT - q0)
            if 0 <= qp < q0:
                dsa_prep(qp, 2 * ss)
                dsa_prep(qp, 2 * ss + 1)
            cur = qt // 2
            if cur < 4:
                return
            ns_t, ns_b = nsel[state["n"] % 2]
            nx_t, nx_b = nselx[state["n"] % 2]
            state["n"] += 1
            job["nx"] = (nx_t, nx_b)
            op("pe", lambda: T.matmul(out=gps[:, 0:2, 0:cur], lhsT=q32[:, ss, qt * 128:(qt + 1) * 128],
                                      rhs=kmz[:, ss, :, 0:cur], start=True, stop=True),
               r=[q32_b, kmz_b], w=[gps_b])
            for hh in range(2):
                g_t, g_b = gt[ss * 2 + hh]
                op("dve", lambda: V.tensor_copy(out=g_t[:, 0:cur], in_=gps[:, hh, 0:cur]), r=[gps_b], w=[g_b])
                op("dve", lambda: V.max(out=top8[:], in_=g_t[:]), r=[g_b], w=[top8_b])
                op("dve", lambda: V.tensor_scalar(out=ns_t[:, hh, :], in0=g_t[:], scalar1=top8[:, 2:3], scalar2=NEG,
                                                   op0=ALU.is_lt, op1=ALU.mult), r=[g_b, top8_b], w=[ns_b])
            op("dve", lambda: V.tensor_copy(out=nx_t[:, :, 0:cur, :],
                                            in_=ns_t[:, :, 0:cur].unsqueeze(3).to_broadcast([128, 2, cur, 128])),
               r=[ns_b], w=[nx_b])

        def make_mask(job):
            qt = job["qt"]
            cur = qt // 2

            def f(si, kt):
                if kt == qt:
                    return (C["ident_h"][:], C["tri_h"][:], [C["ident_h_b"], C["tri_h_b"]])
                if cur >= 4 and kt < 2 * cur:
                    nx_t, nx_b = job["nx"]
                    return (nx_t[:, si, kt // 2, :], C["ident_h"][:], [nx_b, C["ident_h_b"]])
                return None
            return f

        def fin(job, acc_t, acc_b):
            qt, ss = job["qt"], job["ss"]
            o_t, o_b = job["ot"]
            op("dve", lambda: V.reciprocal(out=rcp[:], in_=acc_t[:, 64:130:65]), r=[acc_b], w=[rcp_b])
            for hh in range(2):
                op("dve", lambda: V.tensor_scalar(out=o_t[:, (ss * 2 + hh) * 64:(ss * 2 + hh + 1) * 64],
                                                   in0=acc_t[:, hh * 65:hh * 65 + 64], scalar1=rcp[:, hh:hh + 1],
                                                   scalar2=None, op0=ALU.mult), r=[acc_b, rcp_b], w=[o_b])
            if ss == 1:
                dma("sp", C["mix_d"][qt * 128:(qt + 1) * 128, 0:256], o_t[:], r=[o_b], w=[C["mix_b"]])

        jobs = []
        for qt in range(NT):
            for ss in range(2):
                job = dict(qt=qt, ss=ss, ktiles=list(range(qt + 1)), prep=prep, fin=fin, ot=ot[qt % 2],
                           streams=[dict(q=(qT, q_b, ss, hh), k=(kT, k_b, ss, hh),
                                         v=(vT, v_b, (ss * 2 + hh) * 65, 65), aoff=hh * 65) for hh in range(2)])
                job["maskfn"] = make_mask(job)
                jobs.append(job)
        pipe.run(jobs)


def phase_diff(P, l, C):
    nc, sc = P.nc, P.sc
    op, dma = sc.op, sc.dma
    V, A, G, T = nc.vector, nc.scalar, nc.gpsimd, nc.tensor
    NT = P.NT
    ext = C["ext"]
    lam_init = 0.8 - 0.6 * math.exp(-0.3 * l)
    with ExitStack() as st:
        qT, q_b = load_slices(P, st, "df_q", C["qkT_d"], C["qkT_b"], SL_DQ, 4)
        kT, k_b = load_slices(P, st, "df_k", C["qkT_d"], C["qkT_b"], SL_DK, 4)
        vT, v_b = load_v(P, st, "df_v", C["vtm_d"], C["vtm_b"], 260, 516)
        pipe = AttnPipe(P, st, "df")
        lp, lp_b = P.sb("df_lp", [128, 4, HD], F32, st)
        src = ext["diff_lambda"][0]
        dma("sp", lp[:], bass.AP(tensor=src.tensor, offset=l * 4 * HD, ap=[[0, 128], [HD, 4], [1, HD]]),
            r=[ext["diff_lambda"][1]], w=[lp_b])
        lj, lj_b = P.sb("df_lj", [128, 2, HD], F32, st)
        ls, ls_b = P.sb("df_ls", [128, 2], F32, st)
        nlam, nlam_b = P.sb("df_nlam", [128, 1], F32, st)
        op("dve", lambda: V.tensor_tensor(out=lj[:], in0=lp[:, 0:4:2, :], in1=lp[:, 1:4:2, :], op=ALU.mult),
           r=[lp_b], w=[lj_b])
        op("dve", lambda: V.tensor_reduce(out=ls[:], in_=lj[:], axis=AX.X, op=ALU.add), r=[lj_b], w=[ls_b])
        op("act", lambda: A.activation(out=ls[:], in_=ls[:], func=AF.Exp), r=[ls_b], w=[ls_b])
        op("dve", lambda: V.tensor_tensor(out=nlam[:], in0=ls[:, 1:2], in1=ls[:, 0:1], op=ALU.subtract),
           r=[ls_b], w=[nlam_b])
        op("dve", lambda: V.tensor_scalar(out=nlam[:], in0=nlam[:], scalar1=-lam_init, scalar2=None, op0=ALU.add),
           r=[nlam_b], w=[nlam_b])
        gsub, gsub_b = P.sb("df_gsub", [128, 128], F32, st)
        src = ext["diff_subln_g"][0]
        dma("sp", gsub[:], bass.AP(tensor=src.tensor, offset=l * 128, ap=[[0, 128], [1, 128]]),
            r=[ext["diff_subln_g"][1]], w=[gsub_b])
        op("dve", lambda: V.tensor_scalar(out=gsub[:], in0=gsub[:], scalar1=1.0 - lam_init, scalar2=None, op0=ALU.mult),
           r=[gsub_b], w=[gsub_b])
        rcp, rcp_b = P.sb("df_rcp", [128, 2], F32, st)
        o1, o1_b = P.sb("df_o1", [128, 128], F32, st)
        o2, o2_b = P.sb("df_o2", [128, 128], F32, st)
        junk, junk_b = P.sb("df_junk", [128, 128], F32, st)
        ssq, ssq_b = P.sb("df_ssq", [128, 1], F32, st)
        ot = [P.sb("df_ot%d" % i, [128, 512], BF16, st) for i in range(2)]

        def fin(job, acc_t, acc_b):
            qt, h = job["qt"], job["h"]
            o_t, o_b = job["ot"]
            op("dve", lambda: V.reciprocal(out=rcp[:], in_=acc_t[:, 128:258:129]), r=[acc_b], w=[rcp_b])
            op("dve", lambda: V.tensor_scalar(out=rcp[:, 1:2], in0=rcp[:, 1:2], scalar1=nlam[:, 0:1], scalar2=None,
                                               op0=ALU.mult), r=[rcp_b, nlam_b], w=[rcp_b])
            op("dve", lambda: V.tensor_scalar(out=o1[:], in0=acc_t[:, 0:128], scalar1=rcp[:, 0:1], scalar2=None,
                                               op0=ALU.mult), r=[acc_b, rcp_b], w=[o1_b])
            op("dve", lambda: V.scalar_tensor_tensor(out=o2[:], in0=acc_t[:, 129:257], scalar=rcp[:, 1:2], in1=o1[:],
                                                      op0=ALU.mult, op1=ALU.add), r=[acc_b, rcp_b, o1_b], w=[o2_b])
            op("act", lambda: A.activation(out=junk[:], in_=o2[:], func=AF.Square, accum_out=ssq[:]),
               r=[o2_b], w=[junk_b, ssq_b])
            op("act", lambda: A.activation(out=ssq[:], in_=ssq[:], func=AF.Sqrt, scale=1.0 / 128, bias=EPS),
               r=[ssq_b], w=[ssq_b])
            op("dve", lambda: V.reciprocal(out=ssq[:], in_=ssq[:]), r=[ssq_b], w=[ssq_b])
            op("dve", lambda: V.scalar_tensor_tensor(out=o_t[:, h * 128:(h + 1) * 128], in0=o2[:], scalar=ssq[:, 0:1],
                                                      in1=gsub[:], op0=ALU.mult, op1=ALU.mult),
               r=[o2_b, ssq_b, gsub_b], w=[o_b])
            if h == 3:
                dma("sp", C["mix_d"][qt * 128:(qt + 1) * 128, 256:768], o_t[:], r=[o_b], w=[C["mix_b"]])

        dsa_prep = make_dsa_prep(P, st, l, C)

        q0 = dsa_split(NT)

        def prep(job):
            if job["qt"] >= q0:
                dsa_prep(job["qt"], job["h"])

        jobs = []
        for qt in range(NT):
            for h in range(4):
                jobs.append(dict(qt=qt, h=h, ktiles=list(range(qt + 1)), fin=fin, ot=ot[qt % 2], prep=prep,
                                 maskfn=causal_tri_mask(C["ident_h"], C["ident_h_b"], C["tri_h"], C["tri_h_b"], qt),
                                 streams=[dict(q=(qT, q_b, h, cp), k=(kT, k_b, h, cp),
                                               v=(vT, v_b, h * 129, 129), aoff=cp * 129) for cp in range(2)]))
        pipe.run(jobs)


NI = 22


def dsa_split(NT):
    return max(1, min(NT - 1, int(round(NT * 0.625))))


def make_dsa_prep(P, st, l, C):
    nc, sc = P.nc, P.sc
    op, dma = sc.op, sc.dma
    V, A, G, T = nc.vector, nc.scalar, nc.gpsimd, nc.tensor
    NT, S = P.NT, P.S
    ntop = min(256, S // 4)
    iqT, iq_b = load_slices(P, st, "ds_iq", C["qkT_d"], C["qkT_b"], SL_IQ, 4)
    ikT, ik_b = load_slices(P, st, "ds_ik", C["qkT_d"], C["qkT_b"], SL_IK, 1)
    ips = [P.ps("ds_ips%d" % i, [128, 512], F32, st) for i in range(2)]
    ch_t = [P.sb("ds_ch%d" % i, [128, 512], F32, st) for i in range(3)]
    score, score_b = P.sb("ds_score", [128, S], F32, st)
    junk, junk_b = P.sb("ds_junk", [128, S], BF16, st)
    nmask = [P.sb("ds_nmask%d" % i, [128, S], BF16, st) for i in range(2)]
    pw2, pw2_b = P.sb("ds_pw2", [128, NI + 1], F32, st)
    stp_, stp_b = P.sb("ds_steps", [128, NI + 1], F32, st)
    m1, m1_b = P.sb("ds_m1", [128, 1], F32, st)
    mid, mid_b = P.sb("ds_mid", [128, 1], F32, st)
    cnt, cnt_b = P.sb("ds_cnt", [128, 1], F32, st)
    dd, dd_b = P.sb("ds_dd", [128, 1], F32, st)
    for i in range(NI + 1):
        op("pool", lambda: G.memset(pw2[:, i:i + 1], 2.0 ** -i), w=[pw2_b])
    wabs, wabs_b, wsgn, wsgn_b = C["wabs"], C["wabs_b"], C["wsgn"], C["wsgn_b"]
    triT, triT_b = C["triT_f"], C["triT_f_b"]
    state = {"n": 0, "ip": 0, "ch": 0}

    def prep_qt(qt, part):
        L = (qt + 1) * 128
        if part == 0:
            state["n"] += 1
        nm_t, nm_b = nmask[state["n"] % 2]
        its = [(0, 4), (4, 10), (10, 16), (16, NI)][part]
        if part == 0:
            _indexer(qt, L)
        for i in range(*its):
            _bis(i, L)
        if part == 3:
            op("dve", lambda: V.tensor_tensor(out=mid[:], in0=mid[:], in1=stp_[:, NI:NI + 1], op=ALU.subtract),
               r=[mid_b, stp_b], w=[mid_b])
            op("dve", lambda: V.tensor_scalar(out=nm_t[:, :L], in0=score[:, :L], scalar1=mid[:, 0:1], scalar2=NEG,
                                               op0=ALU.is_lt, op1=ALU.mult), r=[score_b, mid_b], w=[nm_b])
            dma("sp", C["nmask_d"][qt, :, 0:L], nm_t[:, :L], r=[nm_b], w=[C["nmask_b"][qt]])

    def _bis(i, L):
        op("dve", lambda: V.tensor_scalar(out=junk[:, :L], in0=score[:, :L], scalar1=mid[:, 0:1], scalar2=0.0,
                                           op0=ALU.is_ge, op1=ALU.add, accum_out=cnt[:, 0:1]),
           r=[score_b, mid_b], w=[junk_b, cnt_b])
        op("dve", lambda: V.tensor_scalar(out=dd[:], in0=cnt[:], scalar1=ntop - 0.5, scalar2=0.5,
                                           op0=ALU.is_ge, op1=ALU.subtract), r=[cnt_b], w=[dd_b])
        op("dve", lambda: V.scalar_tensor_tensor(out=mid[:], in0=dd[:], scalar=stp_[:, i:i + 1], in1=mid[:],
                                                  op0=ALU.mult, op1=ALU.add), r=[dd_b, stp_b, mid_b], w=[mid_b])

    def _indexer(qt, L):
        for c0 in range(0, L, 512):
            cw = min(512, L - c0)
            for h in range(8):
                ip_t, ip_b = ips[state["ip"] % 2]
                state["ip"] += 1
                rows = slice((h % 2) * 64, (h % 2) * 64 + 64)
                op("pe", lambda: T.matmul(out=ip_t[:, :cw], lhsT=iqT[rows, h // 2, qt * 128:(qt + 1) * 128],
                                          rhs=ikT[rows, 0, c0:c0 + cw], start=True, stop=True),
                   r=[iq_b, ik_b], w=[ip_b])
                c_t, c_b = ch_t[state["ch"] % 3]
                state["ch"] += 1
                op("act", lambda: A.activation(out=c_t[:, :cw], in_=ip_t[:, :cw], func=AF.Relu,
                                               scale=wabs[:, qt, h:h + 1]), r=[ip_b, wabs_b], w=[c_b])
                if h == 0:
                    op("dve", lambda: V.tensor_scalar(out=score[:, c0:c0 + cw], in0=c_t[:, :cw],
                                                       scalar1=wsgn[:, qt, 0:1], scalar2=None, op0=ALU.mult),
                       r=[c_b, wsgn_b], w=[score_b])
                else:
                    op("dve", lambda: V.scalar_tensor_tensor(out=score[:, c0:c0 + cw], in0=c_t[:, :cw],
                                                              scalar=wsgn[:, qt, h:h + 1], in1=score[:, c0:c0 + cw],
                                                              op0=ALU.mult, op1=ALU.add),
                       r=[c_b, wsgn_b, score_b], w=[score_b])
        op("dve", lambda: V.tensor_reduce(out=m1[:], in_=score[:, :L], axis=AX.X, op=ALU.max,
                                          apply_absolute_value=True), r=[score_b], w=[m1_b])
        op("dve", lambda: V.tensor_scalar(out=m1[:], in0=m1[:], scalar1=1.0, scalar2=None, op0=ALU.add),
           r=[m1_b], w=[m1_b])
        op("dve", lambda: V.tensor_scalar(out=stp_[:], in0=pw2[:], scalar1=m1[:, 0:1], scalar2=None, op0=ALU.mult),
           r=[pw2_b, m1_b], w=[stp_b])
        op("dve", lambda: V.tensor_tensor(out=score[:, L - 128:L], in0=score[:, L - 128:L], in1=triT[:], op=ALU.add),
           r=[score_b, triT_b], w=[score_b])
        op("dve", lambda: V.memset(mid[:], 0.0), w=[mid_b])
    return prep_qt


def phase_dsa(P, l, C):
    nc, sc = P.nc, P.sc
    op, dma = sc.op, sc.dma
    V, A, G, T = nc.vector, nc.scalar, nc.gpsimd, nc.tensor
    NT, S = P.NT, P.S
    with ExitStack() as st:
        qT, q_b = load_slices(P, st, "ds_q", C["qkT_d"], C["qkT_b"], SL_SQ, 2)
        kT, k_b = load_slices(P, st, "ds_k", C["qkT_d"], C["qkT_b"], SL_SK, 2)
        vT, v_b = load_v(P, st, "ds_v", C["vtm_d"], C["vtm_b"], 776, 260)
        pipe = AttnPipe(P, st, "ds")
        nmask = [P.sb("ds_nm%d" % i, [128, S], BF16, st) for i in range(2)]
        rcp, rcp_b = P.sb("ds_rcp", [128, 2], F32, st)
        ot = [P.sb("ds_ot%d" % i, [128, 256], BF16, st) for i in range(2)]
        state = {"n": 0}

        def prep(job):
            qt, ss = job["qt"], job["ss"]
            if ss == 1:
                job["nm"] = job["prev"]["nm"]
                return
            L = (qt + 1) * 128
            nm_t, nm_b = nmask[state["n"] % 2]
            state["n"] += 1
            job["nm"] = (nm_t, nm_b)
            dma("sp", nm_t[:, :L], C["nmask_d"][qt, :, 0:L], r=[C["nmask_b"][qt]], w=[nm_b])

        def make_mask(job):
            def f(si, kt):
                nm_t, nm_b = job["nm"]
                return (nm_t[:, kt * 128:(kt + 1) * 128], C["ident_h"][:], [nm_b, C["ident_h_b"]])
            return f

        def fin(job, acc_t, acc_b):
            qt, ss = job["qt"], job["ss"]
            o_t, o_b = job["ot"]
            op("dve", lambda: V.reciprocal(out=rcp[:], in_=acc_t[:, 64:130:65]), r=[acc_b], w=[rcp_b])
            for hh in range(2):
                op("dve", lambda: V.tensor_scalar(out=o_t[:, (ss * 2 + hh) * 64:(ss * 2 + hh + 1) * 64],
                                                   in0=acc_t[:, hh * 65:hh * 65 + 64], scalar1=rcp[:, hh:hh + 1],
                                                   scalar2=None, op0=ALU.mult), r=[acc_b, rcp_b], w=[o_b])
            if ss == 1:
                dma("sp", C["mix_d"][qt * 128:(qt + 1) * 128, 768:1024], o_t[:], r=[o_b], w=[C["mix_b"]])

        jobs = []
        prev = None
        for qt in range(NT):
            for ss in range(2):
                job = dict(qt=qt, ss=ss, ktiles=list(range(qt + 1)), prep=prep, fin=fin, ot=ot[qt % 2], prev=prev,
                           streams=[dict(q=(qT, q_b, ss, hh), k=(kT, k_b, ss, hh),
                                         v=(vT, v_b, (ss * 2 + hh) * 65, 65), aoff=hh * 65) for hh in range(2)])
                job["maskfn"] = make_mask(job)
                jobs.append(job)
                prev = job
        pipe.run(jobs)


def phase_ffn(P, l, C, xin_d, xin_b, xout_d, xout_b):
    nc, sc = P.nc, P.sc
    op, dma = sc.op, sc.dma
    V, A, G, T = nc.vector, nc.scalar, nc.gpsimd, nc.tensor
    NT = P.NT
    ext = C["ext"]
    TG = min(2, NT)
    TW = TG * 128
    NF = D_FF // 128
    ident_h, ident_h_b = C["ident_h"], C["ident_h_b"]
    with ExitStack() as st:
        g_f, g_f_b = P.sb("g_f", [128, 8], F32, st)
        gsrc = ext["ffn_norm_g"][0]
        dma("sp", g_f[:], bass.AP(tensor=gsrc.tensor, offset=l * D, ap=[[1, 128], [128, 8]]),
            r=[ext["ffn_norm_g"][1]], w=[g_f_b], allow_slow_non_contiguous=True)
        wo, wo_b = load_weight(P, st, "w_out_b", ext["w_out"][0][l], ext["w_out"][1], D, D, None, None)
        wg, wg_b = load_weight(P, st, "w_gate_b", ext["w_gate"][0][l], ext["w_gate"][1], D, D_FF, g_f, g_f_b)
        wu, wu_b = load_weight(P, st, "w_up_b", ext["w_up"][0][l], ext["w_up"][1], D, D_FF, g_f, g_f_b)
        wd, wd_b = load_weight(P, st, "w_down_b", ext["w_down"][0][l], ext["w_down"][1], D_FF, D, None, None)
        mxs = [P.sb("e_mx%d" % i, [128, D], BF16, st) for i in range(2)]
        xts = [P.sb("e_xt%d" % i, [128, D], F32, st) for i in range(2)]

        def eload(t):
            rows_ = slice(t * 128, (t + 1) * 128)
            dma("sp", mxs[t % 2][0][:], C["mix_d"][rows_, :], r=[C["mix_b"]], w=[mxs[t % 2][1]])
            dma("sp", xts[t % 2][0][:], xin_d[rows_, :], r=[xin_b], w=[xts[t % 2][1]])
        mixT, mixT_b = P.sb("e_mixT", [128, 8, 128], BF16, st)
        x1, x1_b = P.sb("e_x1", [128, TG, D], F32, st)
        junk, junk_b = P.sb("e_junk", [128, D], BF16, st)
        ssum, ssum_b = P.sb("e_ssum", [128, 1], F32, st)
        hb, hb_b = P.sb("e_hb", [128, D], BF16, st)
        h2T, h2T_b = P.sb("e_h2T", [128, 8, TW], BF16, st)
        sg = [P.sb("e_sg%d" % i, [128, TW], F32, st) for i in range(2)]
        aT, aT_b = P.sb("e_aT", [128, NF, TW], BF16, st)
        outt, outt_b = P.sb("e_out", [128, D], F32, st)
        ptr, ptr_b = P.ps("e_ptr", [128, 1024], BF16, st)
        pp = [P.ps("e_pp%d" % i, [128, 512], F32, st) for i in range(2)]
        gp = [P.ps("e_gp%d" % i, [128, 512], F32, st) for i in range(2)]
        up = [P.ps("e_up%d" % i, [128, 512], F32, st) for i in range(2)]
        npp = 0
        for g in range(NT // TG):
            for ti in range(TG):
                t = g * TG + ti
                rows = slice(t * 128, (t + 1) * 128)
                if t == 0:
                    eload(0)
                if t + 1 < NT:
                    eload(t + 1)
                mx, mx_b = mxs[t % 2]
                xt, xt_b = xts[t % 2]
                for kc in range(8):
                    op("pe", lambda: T.transpose(out=ptr[:, kc * 128:(kc + 1) * 128], in_=mx[:, kc * 128:(kc + 1) * 128],
                                                 identity=ident_h[:]), r=[mx_b, ident_h_b], w=[ptr_b], inc=(kc == 7))
                op("act", lambda: A.copy(out=mixT[:].rearrange("p a b -> p (a b)"), in_=ptr[:]), r=[ptr_b], w=[mixT_b])
                for cc in range(2):
                    p_t, p_b = pp[npp % 2]
                    npp += 1
                    for kc in range(8):
                        op("pe", lambda: T.matmul(out=p_t[:], lhsT=mixT[:, kc, :], rhs=wo[:, kc, cc * 512:(cc + 1) * 512],
                                                  start=(kc == 0), stop=(kc == 7)), r=[mixT_b, wo_b], w=[p_b], inc=(kc == 7))
                    op("dve", lambda: V.tensor_tensor(out=x1[:, ti, cc * 512:(cc + 1) * 512], in0=p_t[:],
                                                      in1=xt[:, cc * 512:(cc + 1) * 512], op=ALU.add),
                       r=[p_b, xt_b], w=[x1_b])
                op("act", lambda: A.activation(out=junk[:], in_=x1[:, ti, :], func=AF.Square, accum_out=ssum[:]),
                   r=[x1_b], w=[junk_b, ssum_b])
                op("act", lambda: A.activation(out=ssum[:], in_=ssum[:], func=AF.Sqrt, scale=1.0 / D, bias=EPS),
                   r=[ssum_b], w=[ssum_b])
                op("dve", lambda: V.reciprocal(out=ssum[:], in_=ssum[:]), r=[ssum_b], w=[ssum_b])
                op("dve", lambda: V.tensor_scalar(out=hb[:], in0=x1[:, ti, :], scalar1=ssum[:, 0:1], scalar2=None,
                                                   op0=ALU.mult), r=[x1_b, ssum_b], w=[hb_b])
                for kc in range(8):
                    op("pe", lambda: T.transpose(out=ptr[:, kc * 128:(kc + 1) * 128], in_=hb[:, kc * 128:(kc + 1) * 128],
                                                 identity=ident_h[:]), r=[hb_b, ident_h_b], w=[ptr_b], inc=(kc == 7))
                op("act", lambda: A.copy(out=h2T[:, :, ti * 128:(ti + 1) * 128],
                                         in_=ptr[:].rearrange("p (a b) -> p a b", b=128)), r=[ptr_b], w=[h2T_b])
            for fc in range(NF):
                g_t, g_b = gp[fc % 2]
                u_t, u_b = up[fc % 2]
                for kc in range(8):
                    op("pe", lambda: T.matmul(out=g_t[:, :TW], lhsT=wg[:, kc, fc * 128:(fc + 1) * 128], rhs=h2T[:, kc, :],
                                              start=(kc == 0), stop=(kc == 7)), r=[wg_b, h2T_b], w=[g_b], inc=(kc == 7))
                for kc in range(8):
                    op("pe", lambda: T.matmul(out=u_t[:, :TW], lhsT=wu[:, kc, fc * 128:(fc + 1) * 128], rhs=h2T[:, kc, :],
                                              start=(kc == 0), stop=(kc == 7)), r=[wu_b, h2T_b], w=[u_b], inc=(kc == 7))
                s_t, s_b = sg[fc % 2]
                op("act", lambda: A.activation(out=s_t[:], in_=g_t[:, :TW], func=AF.Silu), r=[g_b], w=[s_b])
                op("dve", lambda: V.tensor_tensor(out=aT[:, fc, :], in0=u_t[:, :TW], in1=s_t[:], op=ALU.mult),
                   r=[u_b, s_b], w=[aT_b])
            for ti in range(TG):
                t = g * TG + ti
                rows = slice(t * 128, (t + 1) * 128)
                for cc in range(2):
                    p_t, p_b = pp[npp % 2]
                    npp += 1
                    for fc in range(NF):
                        op("pe", lambda: T.matmul(out=p_t[:], lhsT=aT[:, fc, ti * 128:(ti + 1) * 128],
                                                  rhs=wd[:, fc, cc * 512:(cc + 1) * 512], start=(fc == 0), stop=(fc == NF - 1)),
                           r=[aT_b, wd_b], w=[p_b], inc=(fc == NF - 1))
                    op("dve", lambda: V.tensor_tensor(out=outt[:, cc * 512:(cc + 1) * 512], in0=p_t[:],
                                                      in1=x1[:, ti, cc * 512:(cc + 1) * 512], op=ALU.add),
                       r=[p_b, x1_b], w=[outt_b])
                dma("sp", xout_d[rows, :], outt[:], r=[outt_b], w=[xout_b])
```

```python
import math
from contextlib import ExitStack
import numpy as np
import concourse.bass as bass
import concourse.mybir as mybir
from concourse.bass_utils import run_bass_kernel_spmd

F32 = mybir.dt.float32
BF16 = mybir.dt.bfloat16
AF = mybir.ActivationFunctionType
ALU = mybir.AluOpType
AX = mybir.AxisListType

D = 1024
HD = 64
IN_COLS = 3656
D_FF = 2816
EPS = 1e-6
NEG = -30000.0
NSL = 21
VW = 1036
QK_REGIONS = [(0, 512, 0), (768, 1024, 512), (2304, 512, 1536), (3072, 576, 2048)]
SL_MQ, SL_MK, SL_DQ, SL_DK, SL_SQ, SL_SK, SL_IQ, SL_IK = 0, 2, 4, 8, 12, 14, 16, 20


class Buf:
    __slots__ = ("name", "w", "r")

    def __init__(self, name):
        self.name = name
        self.w = {}
        self.r = {}


class Sched:
    NDS = 8

    def __init__(self, nc, ctx):
        self.nc = nc
        self.E = {"pe": nc.tensor, "act": nc.scalar, "dve": nc.vector, "pool": nc.gpsimd, "sp": nc.sync}
        self.semobj = {}
        self.owner = {}
        self.cnt = {}
        for e in ("pe", "act", "dve", "pool"):
            self.semobj[e] = ctx.enter_context(nc.semaphore("s_" + e))
            self.owner[e] = e
            self.cnt[e] = 0
        self.dcnt = {}
        for q in ("sp", "pool"):
            self.dcnt[q] = 0
            for i in range(self.NDS):
                k = "d_%s%d" % (q, i)
                self.semobj[k] = ctx.enter_context(nc.semaphore(k))
                self.owner[k] = None
        self.seen = {e: {} for e in self.E}
        self.nwait = 0
        self.nins = 0

    def _collect(self, eng, r, w, is_dma=False):
        need = {}

        def add(k, v, raw):
            if (not is_dma) and self.owner.get(k) == eng and (eng == "pe" or not raw):
                return
            if need.get(k, 0) < v:
                need[k] = v
        for b in r:
            for k, v in b.w.items():
                add(k, v, True)
        for b in w:
            for k, v in b.w.items():
                add(k, v, False)
            for k, v in b.r.items():
                add(k, v, False)
        return need

    def _wait(self, eng, need):
        sn = self.seen[eng]
        for k, v in need.items():
            if sn.get(k, 0) >= v:
                continue
            own = self.owner.get(k)
            if own is not None:
                assert v <= self.cnt[own], "wait on unmaterialised token %s %d>%d" % (k, v, self.cnt[own])
            self.E[eng].wait_ge(self.semobj[k], v)
            sn[k] = v
            self.nwait += 1

    def _mark(self, key, val, r, w):
        for b in r:
            if b.r.get(key, 0) < val:
                b.r[key] = val
        for b in w:
            b.w = {key: val}
            b.r = {}

    def op(self, eng, fn, r=(), w=(), inc=True):
        self._wait(eng, self._collect(eng, r, w))
        ins = fn()
        if inc:
            self.cnt[eng] += 1
            ins.then_inc(self.semobj[eng], 1)
            val = self.cnt[eng]
        else:
            val = self.cnt[eng] + 1
        self._mark(eng, val, r, w)
        self.nins += 1
        return ins

    def dma(self, q, out, in_, r=(), w=(), **kw):
        i = self.dcnt[q]
        self.dcnt[q] += 1
        key = "d_%s%d" % (q, i % self.NDS)
        val = 16 * (i // self.NDS + 1)
        need = self._collect(q, r, w, is_dma=True)
        if i >= self.NDS and need.get(key, 0) < val - 16:
            need[key] = val - 16
        self._wait(q, need)
        self.E[q].dma_start(out=out, in_=in_, **kw).then_inc(self.semobj[key], 16)
        self._mark(key, val, r, w)
        self.nins += 1

    def barrier(self):
        need = {}
        for e in ("pe", "act", "dve", "pool"):
            if self.cnt[e] > 0:
                need[e] = self.cnt[e]
        for q, n in self.dcnt.items():
            for i in range(max(0, n - self.NDS), n):
                need["d_%s%d" % (q, i % self.NDS)] = 16 * (i // self.NDS + 1)
        for e in ("pe", "act", "dve", "pool", "sp"):
            nd = {k: v for k, v in need.items() if k != e}
            self._wait(e, nd)

    def finish(self):
        need = {}
        for q, n in self.dcnt.items():
            for i in range(max(0, n - self.NDS), n):
                key = "d_%s%d" % (q, i % self.NDS)
                need[key] = 16 * (i // self.NDS + 1)
        self._wait("sp", need)


class Prog:
    def __init__(self, S, depth, debug=()):
        self.S = S
        self.NT = S // 128
        self.NB = S // 256
        self.depth = depth
        self.debug = debug
        self.nc = bass.Bass("TRN2", target_bir_lowering=False)
        self.ctx = ExitStack()
        self.sc = Sched(self.nc, self.ctx)
        self.bufs = {}

    def dram(self, name, shape, dt, kind="Internal"):
        t = self.nc.dram_tensor(name, list(shape), dt, kind=kind).ap()
        return t, Buf(name)

    def _uniq(self, name):
        self.uid = getattr(self, "uid", 0) + 1
        return "%s_u%d" % (name, self.uid)

    def sb(self, name, shape, dt, stack=None):
        t = (stack or self.ctx).enter_context(self.nc.sbuf_tensor(self._uniq(name), list(shape), dt))
        return t, Buf(name)

    def ps(self, name, shape, dt, stack=None):
        t = (stack or self.ctx).enter_context(self.nc.psum_tensor(self._uniq(name), list(shape), dt))
        return t, Buf(name)


def build(S=4096, depth=2, stages=("A", "B", "C", "D", "E"), debug=()):
    P = Prog(S, depth, debug)
    nc, sc = P.nc, P.sc
    NT, NB = P.NT, P.NB
    op, dma = sc.op, sc.dma
    V, A, G, T = nc.vector, nc.scalar, nc.gpsimd, nc.tensor

    ext = {}

    def ein(name, shape):
        ext[name] = P.dram(name, shape, F32, kind="ExternalInput")
        return ext[name]
    x_d, x_b = ein("x", [S, D])
    ein("attn_norm_g", [depth, D])
    ein("w_in", [depth, D, IN_COLS])
    ein("q_norm_g", [depth, 3, HD])
    ein("k_norm_g", [depth, 3, HD])
    ein("diff_lambda", [depth, 4, HD])
    ein("diff_subln_g", [depth, 128])
    ein("w_out", [depth, D, D])
    ein("ffn_norm_g", [depth, D])
    ein("w_gate", [depth, D, D_FF])
    ein("w_up", [depth, D, D_FF])
    ein("w_down", [depth, D_FF, D])
    ein("c_cos", [S, 32])
    ein("c_sin", [S, 32])
    ein("c_ident", [128, 128])
    ein("c_tri", [128, 128])
    ein("c_triT", [128, 128])
    out_d, out_b = P.dram("out", [S, D], F32, kind="ExternalOutput")
    qkT_d, _ = P.dram("qkT", [NSL, 128, S], BF16)
    qkT_b = [Buf("qkT%d" % i) for i in range(NSL)]
    mq32_d, mq32_b = P.dram("mqT32", [2, 128, S], F32)
    vtm_d, vtm_b = P.dram("vtm", [S, VW], BF16)
    mix_d, mix_b = P.dram("mix", [S, D], BF16)
    xmid_d, xmid_b = P.dram("xmid", [S, D], F32)
    nmask_d, _ = P.dram("nmask", [NT, 128, S], BF16)
    nmask_b = [Buf("nmask%d" % i) for i in range(NT)]
    dbg = {}

    ident_f, ident_f_b = P.sb("ident_f", [128, 128], F32)
    ident_h, ident_h_b = P.sb("ident_h", [128, 128], BF16)
    tri_f, tri_f_b = P.sb("tri_f", [128, 128], F32)
    tri_h, tri_h_b = P.sb("tri_h", [128, 128], BF16)
    ones_f, ones_f_b = P.sb("ones_f", [128, 1], F32)
    kmean, kmean_b = P.sb("kmean", [128, 2, NB], F32)
    wabs, wabs_b = P.sb("wabs", [128, NT, 8], F32)
    wsgn, wsgn_b = P.sb("wsgn", [128, NT, 8], F32)

    dma("sp", ident_f[:], ext["c_ident"][0][:, :], r=[ext["c_ident"][1]], w=[ident_f_b])
    dma("sp", tri_f[:], ext["c_tri"][0][:, :], r=[ext["c_tri"][1]], w=[tri_f_b])
    op("dve", lambda: V.tensor_copy(out=ident_h[:], in_=ident_f[:]), r=[ident_f_b], w=[ident_h_b])
    op("dve", lambda: V.tensor_copy(out=tri_h[:], in_=tri_f[:]), r=[tri_f_b], w=[tri_h_b])
    op("dve", lambda: V.memset(ones_f[:], 1.0), w=[ones_f_b])

    triT_f, triT_f_b = P.sb("triT_f", [128, 128], F32)
    dma("sp", triT_f[:], ext["c_triT"][0][:, :], r=[ext["c_triT"][1]], w=[triT_f_b])
    C = dict(ext=ext, qkT_d=qkT_d, qkT_b=qkT_b, mq32_d=mq32_d, mq32_b=mq32_b, vtm_d=vtm_d, vtm_b=vtm_b,
             mix_d=mix_d, mix_b=mix_b, ident_h=ident_h, ident_h_b=ident_h_b, tri_h=tri_h, tri_h_b=tri_h_b,
             kmean=kmean, kmean_b=kmean_b, wabs=wabs, wabs_b=wabs_b, wsgn=wsgn, wsgn_b=wsgn_b,
             triT_f=triT_f, triT_f_b=triT_f_b, nmask_d=nmask_d, nmask_b=nmask_b)
    for l in range(depth):
        xin_d, xin_b = (x_d, x_b) if l == 0 else (xmid_d, xmid_b)
        xout_d, xout_b = (out_d, out_b) if l == depth - 1 else (xmid_d, xmid_b)
        if "A" in stages:
            phase_A(P, l, ext, xin_d, xin_b, qkT_d, qkT_b, mq32_d, mq32_b, vtm_d, vtm_b,
                    ident_f, ident_f_b, ident_h, ident_h_b, ones_f, ones_f_b,
                    kmean, kmean_b, wabs, wabs_b, wsgn, wsgn_b)
        sc.barrier()
        if "B" in stages:
            phase_moba(P, l, C)
            sc.barrier()
        if "C" in stages:
            phase_diff(P, l, C)
            sc.barrier()
        if "D" in stages:
            phase_dsa(P, l, C)
            sc.barrier()
        if "E" in stages:
            phase_ffn(P, l, C, xin_d, xin_b, xout_d, xout_b)
            sc.barrier()
    if "M" in debug:
        with ExitStack() as st_:
            tmpm, tmpm_b = P.sb("dbgm", [128, D], BF16, st_)
            dm, dmb = P.dram("dbg_mix", [S, D], BF16, kind="ExternalOutput")
            for t in range(NT):
                dma("sp", tmpm[:], mix_d[t * 128:(t + 1) * 128, :], r=[mix_b], w=[tmpm_b])
                dma("sp", dm[t * 128:(t + 1) * 128, :], tmpm[:], r=[tmpm_b], w=[dmb])

    if "A" in debug:
        for nm, src, sbuf_, shape in (("kmean", kmean, kmean_b, [128, 2 * NB]),
                                      ("wabs", wabs, wabs_b, [128, NT * 8]),
                                      ("wsgn", wsgn, wsgn_b, [128, NT * 8])):
            d_, b_ = P.dram("dbg_" + nm, shape, F32, kind="ExternalOutput")
            flat = src[:].rearrange("p a b -> p (a b)")
            dma("sp", d_[:, :], flat, r=[sbuf_], w=[b_])
        with ExitStack() as st:
            tmp, tmp_b = P.sb("dbgtmp", [128, S], BF16, st)
            d_, b_ = P.dram("dbg_qkT", [NSL, 128, S], BF16, kind="ExternalOutput")
            for i in range(NSL):
                dma("sp", tmp[:], qkT_d[i, :, :], r=[qkT_b[i]], w=[tmp_b])
                dma("sp", d_[i, :, :], tmp[:], r=[tmp_b], w=[b_])
            tmp2, tmp2_b = P.sb("dbgtmp2", [128, VW], BF16, st)
            d2, b2 = P.dram("dbg_vtm", [S, VW], BF16, kind="ExternalOutput")
            for t in range(NT):
                dma("sp", tmp2[:], vtm_d[t * 128:(t + 1) * 128, :], r=[vtm_b], w=[tmp2_b])
                dma("sp", d2[t * 128:(t + 1) * 128, :], tmp2[:], r=[tmp2_b], w=[b2])
            tmp3, tmp3_b = P.sb("dbgtmp3", [128, S], F32, st)
            d3, b3 = P.dram("dbg_mq32", [2, 128, S], F32, kind="ExternalOutput")
            for i in range(2):
                dma("sp", tmp3[:], mq32_d[i, :, :], r=[mq32_b], w=[tmp3_b])
                dma("sp", d3[i, :, :], tmp3[:], r=[tmp3_b], w=[b3])
    sc.finish()
    P.ctx.close()
    return P


def load_weight(P, st, name, src_d, src_b, rows, cols, g_tile, g_b, engines=("dve", "act")):
    nc, sc = P.nc, P.sc
    KC = rows // 128
    wb, wb_b = P.sb(name, [128, KC, cols], BF16, st)
    CH = 1024
    with ExitStack() as st2:
        NSTG = 4
        stg = [P.sb("%s_stg%d" % (name, i), [128, CH], F32, st2) for i in range(NSTG)]
        n = 0
        for kc in range(KC):
            for c0 in range(0, cols, CH):
                cw = min(CH, cols - c0)
                s_t, s_b = stg[n % NSTG]
                sc.dma("sp", s_t[:, :cw], src_d[kc * 128:(kc + 1) * 128, c0:c0 + cw],
                       r=[src_b], w=[s_b])
                e = engines[n % len(engines)]
                dst = wb[:, kc, c0:c0 + cw]
                if g_tile is None:
                    if e == "act":
                        sc.op(e, lambda: nc.scalar.copy(out=dst, in_=s_t[:, :cw]), r=[s_b], w=[wb_b])
                    else:
                        sc.op(e, lambda: sc.E[e].tensor_copy(out=dst, in_=s_t[:, :cw]), r=[s_b], w=[wb_b])
                else:
                    gs = g_tile[:, kc:kc + 1]
                    if e == "act":
                        sc.op(e, lambda: nc.scalar.activation(out=dst, in_=s_t[:, :cw], func=AF.Copy, scale=gs),
                              r=[s_b, g_b], w=[wb_b])
                    else:
                        sc.op(e, lambda: sc.E[e].tensor_scalar(out=dst, in0=s_t[:, :cw], scalar1=gs, scalar2=None,
                                                               op0=ALU.mult), r=[s_b, g_b], w=[wb_b])
                n += 1
        sc.barrier()
    return wb, wb_b


def phase_A(P, l, ext, xin_d, xin_b, qkT_d, qkT_b, mq32_d, mq32_b, vtm_d, vtm_b,
            ident_f, ident_f_b, ident_h, ident_h_b, ones_f, ones_f_b,
            kmean, kmean_b, wabs, wabs_b, wsgn, wsgn_b):
    nc, sc = P.nc, P.sc
    S, NT = P.S, P.NT
    op, dma = sc.op, sc.dma
    V, A, G, T = nc.vector, nc.scalar, nc.gpsimd, nc.tensor
    TG = min(4, NT)
    with ExitStack() as st:
        g_at, g_at_b = P.sb("g_at", [128, 8], F32, st)
        gsrc = ext["attn_norm_g"][0]
        dma("sp", g_at[:], bass.AP(tensor=gsrc.tensor, offset=l * D, ap=[[1, 128], [128, 8]]),
            r=[ext["attn_norm_g"][1]], w=[g_at_b], allow_slow_non_contiguous=True)
        gqk, gqk_b = P.sb("gqk", [128, 32, HD], F32, st)
        for (nm, gi, h0, nh) in (("q_norm_g", 0, 0, 4), ("k_norm_g", 0, 4, 4), ("q_norm_g", 1, 8, 8),
                                 ("k_norm_g", 1, 16, 8), ("q_norm_g", 2, 24, 4), ("k_norm_g", 2, 28, 4)):
            src = ext[nm][0]
            dma("sp", gqk[:, h0:h0 + nh, :],
                bass.AP(tensor=src.tensor, offset=(l * 3 + gi) * HD, ap=[[0, 128], [0, nh], [1, HD]]),
                r=[ext[nm][1]], w=[gqk_b])
        wsrc = ext["w_in"][0]
        wb, wb_b = load_weight(P, st, "w_in_b", wsrc[l], ext["w_in"][1], D, IN_COLS, g_at, g_at_b)

        if "W" in P.debug:
            d_, b_ = P.dram("dbg_wb", [128, 8, IN_COLS], BF16, kind="ExternalOutput")
            dma("sp", d_[:, :, :], wb[:], r=[wb_b], w=[b_])
        xt = [P.sb("xt%d" % i, [128, D], F32, st) for i in range(3)]
        junk, junk_b = P.sb("junkA", [128, D], BF16, st)
        ssum, ssum_b = P.sb("ssum", [128, 1], F32, st)
        rstd, rstd_b = P.sb("rstd", [128, 1], F32, st)
        hbs = [P.sb("hb%d" % i, [128, D], BF16, st) for i in range(2)]
        hTs = [P.sb("hT%d" % i, [128, 8, 128], BF16, st) for i in range(2)]
        pjs = [P.sb("pj%d" % i, [128, IN_COLS], F32, st) for i in range(2)]
        sq, sq_b = P.sb("sq", [128, 2048], F32, st)
        ss, ss_b = P.sb("ss", [128, 32], F32, st)
        rinv, rinv_b = P.sb("rinv", [128, 32], F32, st)
        nrm, nrm_b = P.sb("nrm", [128, 2624], F32, st)
        rp, rp_b = P.sb("rp", [128, 2624], F32, st)
        t1, t1_b = P.sb("t1", [128, 41 * 32], F32, st)
        t2, t2_b = P.sb("t2", [128, 41 * 32], F32, st)
        t3, t3_b = P.sb("t3", [128, 41 * 32], F32, st)
        t4, t4_b = P.sb("t4", [128, 41 * 32], F32, st)
        css = [P.sb("cs%d" % i, [128, 32], F32, st) for i in range(3)]
        sns = [P.sb("sn%d" % i, [128, 32], F32, st) for i in range(3)]
        qk_tm, qk_tm_b = P.sb("qk_tm", [128, NSL * 128], BF16, st)
        qkst = [P.sb("qkst%d" % i, [128, NSL, TG * 128], BF16, st) for i in range(1)]
        mqst = [P.sb("mqst%d" % i, [128, 2, TG * 128], F32, st) for i in range(1)]
        vt = [P.sb("vt%d" % i, [128, VW], BF16, st) for i in range(2)]
        kps_s, kps_s_b = P.sb("kps_s", [128, 2], F32, st)
        wtmp, wtmp_b = P.sb("wtmp", [128, 8], F32, st)
        pp = [P.ps("ppA%d" % i, [128, 512], F32, st) for i in range(3)]
        ptr = [P.ps("ptrA%d" % i, [128, 1024], BF16, st) for i in range(2)]
        ptf, ptf_b = P.ps("ptfA", [128, 512], F32, st)
        kps, kps_b = P.ps("kpsA", [128, 512], F32, st)

        for i in range(2):
            op("dve", lambda: V.memset(vt[i][0][:], 1.0), w=[vt[i][1]])

        state = {"npp": 0, "nptr": 0}

        pend = []

        def tick():
            if pend:
                pend.pop(0)()

        def load(t):
            x_t, x_tb = xt[t % 3]
            cs, cs_b = css[t % 3]
            sn, sn_b = sns[t % 3]
            rows = slice(t * 128, (t + 1) * 128)
            dma("sp", x_t[:], xin_d[rows, :], r=[xin_b], w=[x_tb])
            dma("sp", cs[:], ext["c_cos"][0][rows, :], r=[ext["c_cos"][1]], w=[cs_b])
            dma("sp", sn[:], ext["c_sin"][0][rows, :], r=[ext["c_sin"][1]], w=[sn_b])

        def front(t):
            npp = state["npp"]
            nptr = state["nptr"]
            hb, hb_b = hbs[t % 2]
            hT, hT_b = hTs[t % 2]
            pj, pj_b = pjs[t % 2]
            cs, cs_b = css[t % 3]
            sn, sn_b = sns[t % 3]
            x_t, x_tb = xt[t % 3]
            rows = slice(t * 128, (t + 1) * 128)
            op("act", lambda: A.activation(out=junk[:], in_=x_t[:], func=AF.Square, accum_out=ssum[:]),
               r=[x_tb], w=[junk_b, ssum_b])
            op("act", lambda: A.activation(out=rstd[:], in_=ssum[:], func=AF.Sqrt, scale=1.0 / D, bias=EPS),
               r=[ssum_b], w=[rstd_b])
            op("dve", lambda: V.reciprocal(out=rstd[:], in_=rstd[:]), r=[rstd_b], w=[rstd_b])
            op("dve", lambda: V.tensor_scalar(out=hb[:], in0=x_t[:], scalar1=rstd[:, 0:1], scalar2=None,
                                               op0=ALU.mult), r=[x_tb, rstd_b], w=[hb_b])
            pt_t, pt_b = ptr[nptr % 2]
            nptr += 1
            for kc in range(8):
                op("pe", lambda: T.transpose(out=pt_t[:, kc * 128:(kc + 1) * 128], in_=hb[:, kc * 128:(kc + 1) * 128],
                                             identity=ident_h[:]), r=[hb_b, ident_h_b], w=[pt_b], inc=(kc == 7))
            op("act", lambda: A.copy(out=hT[:].rearrange("p a b -> p (a b)"), in_=pt_t[:]), r=[pt_b], w=[hT_b])
            chunks = list(enumerate(range(0, IN_COLS, 512)))
            tiles = {}

            def mm(ci, c0):
                cw = min(512, IN_COLS - c0)
                p_t, p_b = pp[state["npp"] % 3]
                state["npp"] += 1
                tiles[ci] = (p_t, p_b)
                for kc in range(8):
                    op("pe", lambda: T.matmul(out=p_t[:, :cw], lhsT=hT[:, kc, :], rhs=wb[:, kc, c0:c0 + cw],
                                              start=(kc == 0), stop=(kc == 7)),
                       r=[hT_b, wb_b], w=[p_b], inc=(kc == 7))

            def make_evac(ci, c0):
                def ev():
                    cw = min(512, IN_COLS - c0)
                    p_t, p_b = tiles[ci]
                    if ci % 2 == 0:
                        op("act", lambda: A.copy(out=pj[:, c0:c0 + cw], in_=p_t[:, :cw]), r=[p_b], w=[pj_b])
                    else:
                        op("dve", lambda: V.tensor_copy(out=pj[:, c0:c0 + cw], in_=p_t[:, :cw]), r=[p_b], w=[pj_b])
                    if ci + 3 < len(chunks):
                        mm(*chunks[ci + 3])
                return ev
            state["npp"] = npp
            for ci, c0 in chunks[:3]:
                mm(ci, c0)
            npp = state["npp"]
            for ci, c0 in chunks:
                pend.append(make_evac(ci, c0))
            state["nptr"] = nptr
            return
            state["npp"] = npp
            state["nptr"] = nptr

        def back(t):
            nptr = state["nptr"]
            rows = slice(t * 128, (t + 1) * 128)
            pj, pj_b = pjs[t % 2]
            cs, cs_b = css[t % 3]
            sn, sn_b = sns[t % 3]
            for (s0, n, d0) in QK_REGIONS[:3]:
                op("act", lambda: A.activation(out=sq[:, d0:d0 + n], in_=pj[:, s0:s0 + n], func=AF.Square),
                   r=[pj_b], w=[sq_b])
            op("dve", lambda: V.tensor_reduce(out=ss[:], in_=sq[:].rearrange("p (h d) -> p h d", d=HD),
                                              axis=AX.X, op=ALU.add), r=[sq_b], w=[ss_b])
            op("act", lambda: A.activation(out=rinv[:], in_=ss[:], func=AF.Sqrt, scale=1.0 / HD, bias=EPS),
               r=[ss_b], w=[rinv_b])
            op("dve", lambda: V.reciprocal(out=rinv[:], in_=rinv[:]), r=[rinv_b], w=[rinv_b])
            tick()
            for (s0, n, d0) in QK_REGIONS[:3]:
                nh = n // HD
                h0 = d0 // HD
                op("dve", lambda: V.tensor_tensor(
                    out=nrm[:, d0:d0 + n].rearrange("p (h d) -> p h d", d=HD),
                    in0=pj[:, s0:s0 + n].rearrange("p (h d) -> p h d", d=HD),
                    in1=rinv[:, h0:h0 + nh].unsqueeze(2).to_broadcast([128, nh, HD]), op=ALU.mult),
                    r=[pj_b, rinv_b], w=[nrm_b])
            op("dve", lambda: V.tensor_tensor(out=nrm[:, 0:2048], in0=nrm[:, 0:2048],
                                              in1=gqk[:].rearrange("p h d -> p (h d)"), op=ALU.mult),
               r=[nrm_b, gqk_b], w=[nrm_b])
            tick()
            op("act", lambda: A.copy(out=nrm[:, 2048:2624], in_=pj[:, 3072:3648]), r=[pj_b], w=[nrm_b])
            nv = nrm[:].rearrange("p (h two d) -> p h two d", two=2, d=32)
            rv = rp[:].rearrange("p (h two d) -> p h two d", two=2, d=32)
            x1, x2 = nv[:, :, 0, :], nv[:, :, 1, :]
            cb = cs[:].unsqueeze(1).to_broadcast([128, 41, 32])
            sbb = sn[:].unsqueeze(1).to_broadcast([128, 41, 32])
            tv = [tt[:].rearrange("p (h d) -> p h d", d=32) for tt in (t1, t2, t3, t4)]
            op("dve", lambda: V.tensor_tensor(out=tv[0], in0=x1, in1=cb, op=ALU.mult), r=[nrm_b, cs_b], w=[t1_b])
            tick()
            op("dve", lambda: V.tensor_tensor(out=tv[1], in0=x2, in1=sbb, op=ALU.mult), r=[nrm_b, sn_b], w=[t2_b])
            tick()
            op("dve", lambda: V.tensor_tensor(out=tv[2], in0=x1, in1=sbb, op=ALU.mult), r=[nrm_b, sn_b], w=[t3_b])
            tick()
            op("dve", lambda: V.tensor_tensor(out=tv[3], in0=x2, in1=cb, op=ALU.mult), r=[nrm_b, cs_b], w=[t4_b])
            tick()
            op("dve", lambda: V.tensor_tensor(out=rv[:, :, 0, :], in0=tv[0], in1=tv[1], op=ALU.subtract),
               r=[t1_b, t2_b], w=[rp_b])
            op("dve", lambda: V.tensor_tensor(out=rv[:, :, 1, :], in0=tv[2], in1=tv[3], op=ALU.add),
               r=[t3_b, t4_b], w=[rp_b])
            op("act", lambda: A.copy(out=qk_tm[:, 0:2624], in_=rp[:, :]), r=[rp_b], w=[qk_tm_b])
            tick()
            op("pool", lambda: G.tensor_copy(out=qk_tm[:, 2624:2688], in_=rp[:, 2560:2624]), r=[rp_b], w=[qk_tm_b])
            g = t // TG
            ti = t % TG
            qs_t, qs_b = qkst[0]
            for j0 in range(0, NSL, 8):
                nj = min(8, NSL - j0)
                pt_t, pt_b = ptr[nptr % 2]
                nptr += 1
                for j in range(nj):
                    op("pe", lambda: T.transpose(out=pt_t[:, j * 128:(j + 1) * 128],
                                                 in_=qk_tm[:, (j0 + j) * 128:(j0 + j + 1) * 128], identity=ident_h[:]),
                       r=[qk_tm_b, ident_h_b], w=[pt_b], inc=(j == nj - 1))
                src = pt_t[:, :nj * 128].rearrange("p (a b) -> p a b", b=128)
                dst = qs_t[:, j0:j0 + nj, ti * 128:(ti + 1) * 128]
                if (j0 // 8) % 2 == 0:
                    op("act", lambda: A.copy(out=dst, in_=src), r=[pt_b], w=[qs_b])
                else:
                    op("dve", lambda: V.tensor_copy(out=dst, in_=src), r=[pt_b], w=[qs_b])
            ms_t, ms_b = mqst[0]
            for j in range(2):
                op("pe", lambda: T.transpose(out=ptf[:, j * 128:(j + 1) * 128], in_=rp[:, j * 128:(j + 1) * 128],
                                             identity=ident_f[:]), r=[rp_b, ident_f_b], w=[ptf_b], inc=(j == 1))
            op("dve", lambda: V.tensor_copy(out=ms_t[:, :, ti * 128:(ti + 1) * 128],
                                            in_=ptf[:, 0:256].rearrange("p (a b) -> p a b", b=128)), r=[ptf_b], w=[ms_b])
            if ti == TG - 1:
                c0 = g * TG * 128
                for j in range(NSL):
                    dma("sp", qkT_d[j, :, c0:c0 + TG * 128], qs_t[:, j, :], r=[qs_b], w=[qkT_b[j]])
                for j in range(2):
                    dma("sp", mq32_d[j, :, c0:c0 + TG * 128], ms_t[:, j, :], r=[ms_b], w=[mq32_b])
            for j in range(2):
                op("pe", lambda: T.matmul(out=kps[:, j:j + 1], lhsT=rp[:, 256 + j * 128:256 + (j + 1) * 128],
                                          rhs=ones_f[:, 0:1], start=True, stop=True),
                   r=[rp_b, ones_f_b], w=[kps_b], inc=(j == 1))
            b = t // 2
            if t % 2 == 0:
                op("dve", lambda: V.tensor_copy(out=kmean[:, :, b], in_=kps[:, 0:2]), r=[kps_b], w=[kmean_b])
            else:
                op("dve", lambda: V.tensor_copy(out=kps_s[:], in_=kps[:, 0:2]), r=[kps_b], w=[kps_s_b])
                op("dve", lambda: V.tensor_tensor(out=kmean[:, :, b], in0=kmean[:, :, b], in1=kps_s[:], op=ALU.add),
                   r=[kps_s_b, kmean_b], w=[kmean_b])
            v_t, v_b = vt[t % 2]
            op("act", lambda: A.copy(out=v_t[:, 0:260].rearrange("p (h d) -> p h d", d=65)[:, :, 0:64],
                                     in_=pj[:, 512:768].rearrange("p (h d) -> p h d", d=64)), r=[pj_b], w=[v_b])
            op("act", lambda: A.copy(out=v_t[:, 260:776].rearrange("p (h d) -> p h d", d=129)[:, :, 0:128],
                                     in_=pj[:, 1792:2304].rearrange("p (h d) -> p h d", d=128)),
               r=[pj_b], w=[v_b])
            op("act", lambda: A.copy(out=v_t[:, 776:1036].rearrange("p (h d) -> p h d", d=65)[:, :, 0:64],
                                     in_=pj[:, 2816:3072].rearrange("p (h d) -> p h d", d=64)), r=[pj_b], w=[v_b])
            dma("sp", vtm_d[rows, :], v_t[:], r=[v_b], w=[vtm_b])
            op("dve", lambda: V.tensor_scalar(out=wtmp[:], in0=pj[:, 3648:3656], scalar1=8.0 ** -1.5, scalar2=None,
                                               op0=ALU.mult), r=[pj_b], w=[wtmp_b])
            op("dve", lambda: V.scalar_tensor_tensor(out=wabs[:, t, :], in0=wtmp[:], scalar=-1.0, in1=wtmp[:],
                                                      op0=ALU.mult, op1=ALU.max), r=[wtmp_b], w=[wabs_b])
            op("act", lambda: A.activation(out=wsgn[:, t, :], in_=wtmp[:], func=AF.Sign), r=[wtmp_b], w=[wsgn_b])
            while pend:
                tick()
            state["nptr"] = nptr

        load(0)
        if NT > 1:
            load(1)
        front(0)
        while pend:
            tick()
        for t in range(NT):
            if t + 2 < NT:
                load(t + 2)
            if t + 1 < NT:
                front(t + 1)
            back(t)


def rope_tables_np(S):
    inv = (1.0 / (np.float32(10000.0) ** (np.arange(0, HD, 2, dtype=np.float32) / np.float32(HD)))).astype(np.float32)
    ang = np.arange(S, dtype=np.float32)[:, None] * inv[None, :]
    return np.cos(ang).astype(np.float32), np.sin(ang).astype(np.float32)


def const_inputs(S):
    cos, sin = rope_tables_np(S)
    k = np.arange(128)[:, None]
    q = np.arange(128)[None, :]
    tri = np.where(k <= q, 0.0, NEG).astype(np.float32)
    triT = np.where(k.T <= q.T, 0.0, -1e30).astype(np.float32)
    return {"c_cos": cos, "c_sin": sin, "c_ident": np.eye(128, dtype=np.float32), "c_tri": tri, "c_triT": triT}


_PROG = {}


def kernel(**inputs):
    x = np.ascontiguousarray(inputs["x"], dtype=np.float32)
    B, S, _ = x.shape
    depth = inputs["w_in"].shape[0]
    key = (S, depth)
    if key not in _PROG:
        _PROG[key] = build(S, depth)
    P = _PROG[key]
    consts = const_inputs(S)
    shared = {k: np.ascontiguousarray(v, dtype=np.float32) for k, v in inputs.items() if k != "x"}
    in_maps = []
    for b in range(B):
        m = {"x": x[b]}
        m.update(shared)
        m.update(consts)
        in_maps.append(m)
    res = run_bass_kernel_spmd(P.nc, in_maps, core_ids=list(range(B)))
    return np.stack([np.asarray(r["out"]) for r in res.results], axis=0).astype(np.float32)


def load_slices(P, st, name, qkT_d, qkT_b, s0, n, dt=BF16):
    t, b = P.sb(name, [128, n, P.S], dt, st)
    for j in range(n):
        P.sc.dma("sp", t[:, j, :], qkT_d[s0 + j, :, :], r=[qkT_b[s0 + j] if isinstance(qkT_b, list) else qkT_b], w=[b])
    return t, b


def load_v(P, st, name, vtm_d, vtm_b, c0, w):
    NT = P.NT
    t, b = P.sb(name, [128, NT, w], BF16, st)
    for t0 in range(0, NT, 8):
        n = min(8, NT - t0)
        src = vtm_d[t0 * 128:(t0 + n) * 128, c0:c0 + w].rearrange("(t p) c -> p t c", p=128)
        P.sc.dma("sp", t[:, t0:t0 + n, :], src, r=[vtm_b], w=[b])
    return t, b


class AttnPipe:
    def __init__(self, P, st, tag, n_st=4):
        self.P = P
        self.n_st = n_st
        self.stp = [P.ps("st%s%d" % (tag, i), [128, 512], F32, st) for i in range(n_st)]
        self.ptp = [P.sb("pt%s%d" % (tag, i), [128, 512], BF16, st) for i in range(4)]
        self.acc = [P.ps("acc%s%d" % (tag, i), [128, 512], F32, st) for i in range(2)]
        self.n = 0
        self.nacc = 0
        self.pending = None

    def run(self, jobs):
        units = []
        for job in jobs:
            kts = job["ktiles"]
            chunks = [kts[i:i + 4] for i in range(0, len(kts), 4)]
            for ci, ch in enumerate(chunks):
                for si in range(len(job["streams"])):
                    units.append(dict(job=job, ci=ci, si=si, ch=ch, first=(ci == 0), last=(ci == len(chunks) - 1)))
        if not units:
            return
        LA = 2
        for j in range(min(LA, len(units))):
            self._qk(units[j])
        for i, u in enumerate(units):
            if i + LA < len(units):
                self._qk(units[i + LA])
            self._exp_pv(u)

    def _qk(self, u):
        P = self.P
        nc, sc = P.nc, P.sc
        T = nc.tensor
        job = u["job"]
        if u["ci"] == 0 and u["si"] == 0:
            if job.get("prep"):
                job["prep"](job)
            job["acc"] = self.acc[self.nacc % 2]
            self.nacc += 1
        s = job["streams"][u["si"]]
        qt = job["qt"]
        st_t, st_b = self.stp[self.n % self.n_st]
        u["st"] = (st_t, st_b)
        u["pt"] = self.ptp[self.n % 4]
        self.n += 1
        qT, q_b, qsl, qh = s["q"]
        kT, k_b, ksl, kh = s["k"]
        qr = slice(qh * 64, qh * 64 + 64)
        kr = slice(kh * 64, kh * 64 + 64)
        n = len(u["ch"])
        for i, kt in enumerate(u["ch"]):
            m = job["maskfn"](u["si"], kt) if job.get("maskfn") else None
            o = st_t[:, i * 128:(i + 1) * 128]
            sc.op("pe", lambda: T.matmul(out=o, lhsT=kT[kr, ksl, kt * 128:(kt + 1) * 128],
                                         rhs=qT[qr, qsl, qt * 128:(qt + 1) * 128], start=True, stop=(m is None)),
                  r=[q_b, k_b], w=[st_b], inc=(m is None and i == n - 1))
            if m is not None:
                ml, mr, mb = m
                sc.op("pe", lambda: T.matmul(out=o, lhsT=ml, rhs=mr, start=False, stop=True, skip_group_check=True),
                      r=mb, w=[st_b], inc=(i == n - 1))

    def _exp_pv(self, u):
        P = self.P
        nc, sc = P.nc, P.sc
        T, A = nc.tensor, nc.scalar
        job = u["job"]
        s = job["streams"][u["si"]]
        st_t, st_b = u["st"]
        pt_t, pt_b = u["pt"]
        acc_t, acc_b = job["acc"]
        n = len(u["ch"])
        sc.op("act", lambda: A.activation(out=pt_t[:, :n * 128], in_=st_t[:, :n * 128], func=AF.Exp, scale=0.125),
              r=[st_b], w=[pt_b])
        vT, v_b, voff, vw = s["v"]
        ao = s["aoff"]
        nstreams = len(job["streams"])
        for i, kt in enumerate(u["ch"]):
            is_first = u["first"] and i == 0
            is_last = u["last"] and i == n - 1
            sc.op("pe", lambda: T.matmul(out=acc_t[:, ao:ao + vw], lhsT=pt_t[:, i * 128:(i + 1) * 128],
                                         rhs=vT[:, kt, voff:voff + vw], start=(is_first and u["si"] == 0),
                                         stop=is_last, skip_group_check=True),
                  r=[pt_b, v_b], w=[acc_b], inc=(is_last and u["si"] == nstreams - 1))
        if u["last"] and u["si"] == nstreams - 1:
            job["fin"](job, acc_t, acc_b)


def causal_tri_mask(ident_h, ident_h_b, tri_h, tri_h_b, qt):
    def f(si, kt):
        if kt == qt:
            return (ident_h[:], tri_h[:], [ident_h_b, tri_h_b])
        return None
    return f


def phase_moba(P, l, C):
    nc, sc = P.nc, P.sc
    op, dma = sc.op, sc.dma
    V, A, G, T = nc.vector, nc.scalar, nc.gpsimd, nc.tensor
    NT, NB = P.NT, P.NB
    with ExitStack() as st:
        qT, q_b = load_slices(P, st, "mo_q", C["qkT_d"], C["qkT_b"], SL_MQ, 2)
        kT, k_b = load_slices(P, st, "mo_k", C["qkT_d"], C["qkT_b"], SL_MK, 2)
        q32, q32_b = load_slices(P, st, "mo_q32", C["mq32_d"], C["mq32_b"], 0, 2, F32)
        vT, v_b = load_v(P, st, "mo_v", C["vtm_d"], C["vtm_b"], 0, 260)
        pipe = AttnPipe(P, st, "mo", n_st=3)
        dsa_prep = make_dsa_prep(P, st, l, C)
        q0 = dsa_split(NT)
        gps, gps_b = P.ps("mo_gps", [128, 32, 16], F32, st)
        gt = [P.sb("mo_gt%d" % i, [128, 16], F32, st) for i in range(4)]
        top8, top8_b = P.sb("mo_top8", [128, 8], F32, st)
        nsel = [P.sb("mo_nsel%d" % i, [128, 2, 16], BF16, st) for i in range(2)]
        nselx = [P.sb("mo_nselx%d" % i, [128, 2, 16, 128], BF16, st) for i in range(2)]
        rcp, rcp_b = P.sb("mo_rcp", [128, 2], F32, st)
        ot = [P.sb("mo_ot%d" % i, [128, 256], BF16, st) for i in range(2)]
        for i in range(4):
            op("dve", lambda: V.memset(gt[i][0][:], -1e30), w=[gt[i][1]])
        kmean, kmean_b = C["kmean"], C["kmean_b"]
        kmz, kmz_b = P.sb("mo_kmz", [128, 2, 2, NB], F32, st)
        op("dve", lambda: V.memset(kmz[:], 0.0), w=[kmz_b])
        for hh in range(2):
            rows = slice(hh * 64, hh * 64 + 64)
            op("dve", lambda: V.tensor_copy(out=kmz[rows, :, hh, :], in_=kmean[rows, :, :]), r=[kmean_b], w=[kmz_b])
        state = {"n": 0}

        def prep(job):
            qt, ss = job["qt"], job["ss"]
            qp = qt - (NT - q0)
            if 0 <= qp < q0:
                dsa_prep(qp, 2 * ss)
                dsa_prep(qp, 2 * ss + 1)
            cur = qt // 2
            if cur < 4:
                return
            ns_t, ns_b = nsel[state["n"] % 2]
            nx_t, nx_b = nselx[state["n"] % 2]
            state["n"] += 1
            job["nx"] = (nx_t, nx_b)
            op("pe", lambda: T.matmul(out=gps[:, 0:2, 0:cur], lhsT=q32[:, ss, qt * 128:(qt + 1) * 128],
                                      rhs=kmz[:, ss, :, 0:cur], start=True, stop=True),
               r=[q32_b, kmz_b], w=[gps_b])
            for hh in range(2):
                g_t, g_b = gt[ss * 2 + hh]
                op("dve", lambda: V.tensor_copy(out=g_t[:, 0:cur], in_=gps[:, hh, 0:cur]), r=[gps_b], w=[g_b])
                op("dve", lambda: V.max(out=top8[:], in_=g_t[:]), r=[g_b], w=[top8_b])
                op("dve", lambda: V.tensor_scalar(out=ns_t[:, hh, :], in0=g_t[:], scalar1=top8[:, 2:3], scalar2=NEG,
                                                   op0=ALU.is_lt, op1=ALU.mult), r=[g_b, top8_b], w=[ns_b])
            op("dve", lambda: V.tensor_copy(out=nx_t[:, :, 0:cur, :],
                                            in_=ns_t[:, :, 0:cur].unsqueeze(3).to_broadcast([128, 2, cur, 128])),
               r=[ns_b], w=[nx_b])

        def make_mask(job):
            qt = job["qt"]
            cur = qt // 2

            def f(si, kt):
                if kt == qt:
                    return (C["ident_h"][:], C["tri_h"][:], [C["ident_h_b"], C["tri_h_b"]])
                if cur >= 4 and kt < 2 * cur:
                    nx_t, nx_b = job["nx"]
                    return (nx_t[:, si, kt // 2, :], C["ident_h"][:], [nx_b, C["ident_h_b"]])
                return None
            return f

        def fin(job, acc_t, acc_b):
            qt, ss = job["qt"], job["ss"]
            o_t, o_b = job["ot"]
            op("dve", lambda: V.reciprocal(out=rcp[:], in_=acc_t[:, 64:130:65]), r=[acc_b], w=[rcp_b])
            for hh in range(2):
                op("dve", lambda: V.tensor_scalar(out=o_t[:, (ss * 2 + hh) * 64:(ss * 2 + hh + 1) * 64],
                                                   in0=acc_t[:, hh * 65:hh * 65 + 64], scalar1=rcp[:, hh:hh + 1],
                                                   scalar2=None, op0=ALU.mult), r=[acc_b, rcp_b], w=[o_b])
            if ss == 1:
                dma("sp", C["mix_d"][qt * 128:(qt + 1) * 128, 0:256], o_t[:], r=[o_b], w=[C["mix_b"]])

        jobs = []
        for qt in range(NT):
            for ss in range(2):
                job = dict(qt=qt, ss=ss, ktiles=list(range(qt + 1)), prep=prep, fin=fin, ot=ot[qt % 2],
                           streams=[dict(q=(qT, q_b, ss, hh), k=(kT, k_b, ss, hh),
                                         v=(vT, v_b, (ss * 2 + hh) * 65, 65), aoff=hh * 65) for hh in range(2)])
                job["maskfn"] = make_mask(job)
                jobs.append(job)
        pipe.run(jobs)


def phase_diff(P, l, C):
    nc, sc = P.nc, P.sc
    op, dma = sc.op, sc.dma
    V, A, G, T = nc.vector, nc.scalar, nc.gpsimd, nc.tensor
    NT = P.NT
    ext = C["ext"]
    lam_init = 0.8 - 0.6 * math.exp(-0.3 * l)
    with ExitStack() as st:
        qT, q_b = load_slices(P, st, "df_q", C["qkT_d"], C["qkT_b"], SL_DQ, 4)
        kT, k_b = load_slices(P, st, "df_k", C["qkT_d"], C["qkT_b"], SL_DK, 4)
        vT, v_b = load_v(P, st, "df_v", C["vtm_d"], C["vtm_b"], 260, 516)
        pipe = AttnPipe(P, st, "df")
        lp, lp_b = P.sb("df_lp", [128, 4, HD], F32, st)
        src = ext["diff_lambda"][0]
        dma("sp", lp[:], bass.AP(tensor=src.tensor, offset=l * 4 * HD, ap=[[0, 128], [HD, 4], [1, HD]]),
            r=[ext["diff_lambda"][1]], w=[lp_b])
        lj, lj_b = P.sb("df_lj", [128, 2, HD], F32, st)
        ls, ls_b = P.sb("df_ls", [128, 2], F32, st)
        nlam, nlam_b = P.sb("df_nlam", [128, 1], F32, st)
        op("dve", lambda: V.tensor_tensor(out=lj[:], in0=lp[:, 0:4:2, :], in1=lp[:, 1:4:2, :], op=ALU.mult),
           r=[lp_b], w=[lj_b])
        op("dve", lambda: V.tensor_reduce(out=ls[:], in_=lj[:], axis=AX.X, op=ALU.add), r=[lj_b], w=[ls_b])
        op("act", lambda: A.activation(out=ls[:], in_=ls[:], func=AF.Exp), r=[ls_b], w=[ls_b])
        op("dve", lambda: V.tensor_tensor(out=nlam[:], in0=ls[:, 1:2], in1=ls[:, 0:1], op=ALU.subtract),
           r=[ls_b], w=[nlam_b])
        op("dve", lambda: V.tensor_scalar(out=nlam[:], in0=nlam[:], scalar1=-lam_init, scalar2=None, op0=ALU.add),
           r=[nlam_b], w=[nlam_b])
        gsub, gsub_b = P.sb("df_gsub", [128, 128], F32, st)
        src = ext["diff_subln_g"][0]
        dma("sp", gsub[:], bass.AP(tensor=src.tensor, offset=l * 128, ap=[[0, 128], [1, 128]]),
            r=[ext["diff_subln_g"][1]], w=[gsub_b])
        op("dve", lambda: V.tensor_scalar(out=gsub[:], in0=gsub[:], scalar1=1.0 - lam_init, scalar2=None, op0=ALU.mult),
           r=[gsub_b], w=[gsub_b])
        rcp, rcp_b = P.sb("df_rcp", [128, 2], F32, st)
        o1, o1_b = P.sb("df_o1", [128, 128], F32, st)
        o2, o2_b = P.sb("df_o2", [128, 128], F32, st)
        junk, junk_b = P.sb("df_junk", [128, 128], F32, st)
        ssq, ssq_b = P.sb("df_ssq", [128, 1], F32, st)
        ot = [P.sb("df_ot%d" % i, [128, 512], BF16, st) for i in range(2)]

        def fin(job, acc_t, acc_b):
            qt, h = job["qt"], job["h"]
            o_t, o_b = job["ot"]
            op("dve", lambda: V.reciprocal(out=rcp[:], in_=acc_t[:, 128:258:129]), r=[acc_b], w=[rcp_b])
            op("dve", lambda: V.tensor_scalar(out=rcp[:, 1:2], in0=rcp[:, 1:2], scalar1=nlam[:, 0:1], scalar2=None,
                                               op0=ALU.mult), r=[rcp_b, nlam_b], w=[rcp_b])
            op("dve", lambda: V.tensor_scalar(out=o1[:], in0=acc_t[:, 0:128], scalar1=rcp[:, 0:1], scalar2=None,
                                               op0=ALU.mult), r=[acc_b, rcp_b], w=[o1_b])
            op("dve", lambda: V.scalar_tensor_tensor(out=o2[:], in0=acc_t[:, 129:257], scalar=rcp[:, 1:2], in1=o1[:],
                                                      op0=ALU.mult, op1=ALU.add), r=[acc_b, rcp_b, o1_b], w=[o2_b])
            op("act", lambda: A.activation(out=junk[:], in_=o2[:], func=AF.Square, accum_out=ssq[:]),
               r=[o2_b], w=[junk_b, ssq_b])
            op("act", lambda: A.activation(out=ssq[:], in_=ssq[:], func=AF.Sqrt, scale=1.0 / 128, bias=EPS),
               r=[ssq_b], w=[ssq_b])
            op("dve", lambda: V.reciprocal(out=ssq[:], in_=ssq[:]), r=[ssq_b], w=[ssq_b])
            op("dve", lambda: V.scalar_tensor_tensor(out=o_t[:, h * 128:(h + 1) * 128], in0=o2[:], scalar=ssq[:, 0:1],
                                                      in1=gsub[:], op0=ALU.mult, op1=ALU.mult),
               r=[o2_b, ssq_b, gsub_b], w=[o_b])
            if h == 3:
                dma("sp", C["mix_d"][qt * 128:(qt + 1) * 128, 256:768], o_t[:], r=[o_b], w=[C["mix_b"]])

        dsa_prep = make_dsa_prep(P, st, l, C)

        q0 = dsa_split(NT)

        def prep(job):
            if job["qt"] >= q0:
                dsa_prep(job["qt"], job["h"])

        jobs = []
        for qt in range(NT):
            for h in range(4):
                jobs.append(dict(qt=qt, h=h, ktiles=list(range(qt + 1)), fin=fin, ot=ot[qt % 2], prep=prep,
                                 maskfn=causal_tri_mask(C["ident_h"], C["ident_h_b"], C["tri_h"], C["tri_h_b"], qt),
                                 streams=[dict(q=(qT, q_b, h, cp), k=(kT, k_b, h, cp),
                                               v=(vT, v_b, h * 129, 129), aoff=cp * 129) for cp in range(2)]))
        pipe.run(jobs)


NI = 22


def dsa_split(NT):
    return max(1, min(NT - 1, int(round(NT * 0.625))))


def make_dsa_prep(P, st, l, C):
    nc, sc = P.nc, P.sc
    op, dma = sc.op, sc.dma
    V, A, G, T = nc.vector, nc.scalar, nc.gpsimd, nc.tensor
    NT, S = P.NT, P.S
    ntop = min(256, S // 4)
    iqT, iq_b = load_slices(P, st, "ds_iq", C["qkT_d"], C["qkT_b"], SL_IQ, 4)
    ikT, ik_b = load_slices(P, st, "ds_ik", C["qkT_d"], C["qkT_b"], SL_IK, 1)
    ips = [P.ps("ds_ips%d" % i, [128, 512], F32, st) for i in range(2)]
    ch_t = [P.sb("ds_ch%d" % i, [128, 512], F32, st) for i in range(3)]
    score, score_b = P.sb("ds_score", [128, S], F32, st)
    junk, junk_b = P.sb("ds_junk", [128, S], BF16, st)
    nmask = [P.sb("ds_nmask%d" % i, [128, S], BF16, st) for i in range(2)]
    pw2, pw2_b = P.sb("ds_pw2", [128, NI + 1], F32, st)
    stp_, stp_b = P.sb("ds_steps", [128, NI + 1], F32, st)
    m1, m1_b = P.sb("ds_m1", [128, 1], F32, st)
    mid, mid_b = P.sb("ds_mid", [128, 1], F32, st)
    cnt, cnt_b = P.sb("ds_cnt", [128, 1], F32, st)
    dd, dd_b = P.sb("ds_dd", [128, 1], F32, st)
    for i in range(NI + 1):
        op("pool", lambda: G.memset(pw2[:, i:i + 1], 2.0 ** -i), w=[pw2_b])
    wabs, wabs_b, wsgn, wsgn_b = C["wabs"], C["wabs_b"], C["wsgn"], C["wsgn_b"]
    triT, triT_b = C["triT_f"], C["triT_f_b"]
    state = {"n": 0, "ip": 0, "ch": 0}

    def prep_qt(qt, part):
        L = (qt + 1) * 128
        if part == 0:
            state["n"] += 1
        nm_t, nm_b = nmask[state["n"] % 2]
        its = [(0, 4), (4, 10), (10, 16), (16, NI)][part]
        if part == 0:
            _indexer(qt, L)
        for i in range(*its):
            _bis(i, L)
        if part == 3:
            op("dve", lambda: V.tensor_tensor(out=mid[:], in0=mid[:], in1=stp_[:, NI:NI + 1], op=ALU.subtract),
               r=[mid_b, stp_b], w=[mid_b])
            op("dve", lambda: V.tensor_scalar(out=nm_t[:, :L], in0=score[:, :L], scalar1=mid[:, 0:1], scalar2=NEG,
                                               op0=ALU.is_lt, op1=ALU.mult), r=[score_b, mid_b], w=[nm_b])
            dma("sp", C["nmask_d"][qt, :, 0:L], nm_t[:, :L], r=[nm_b], w=[C["nmask_b"][qt]])

    def _bis(i, L):
        op("dve", lambda: V.tensor_scalar(out=junk[:, :L], in0=score[:, :L], scalar1=mid[:, 0:1], scalar2=0.0,
                                           op0=ALU.is_ge, op1=ALU.add, accum_out=cnt[:, 0:1]),
           r=[score_b, mid_b], w=[junk_b, cnt_b])
        op("dve", lambda: V.tensor_scalar(out=dd[:], in0=cnt[:], scalar1=ntop - 0.5, scalar2=0.5,
                                           op0=ALU.is_ge, op1=ALU.subtract), r=[cnt_b], w=[dd_b])
        op("dve", lambda: V.scalar_tensor_tensor(out=mid[:], in0=dd[:], scalar=stp_[:, i:i + 1], in1=mid[:],
                                                  op0=ALU.mult, op1=ALU.add), r=[dd_b, stp_b, mid_b], w=[mid_b])

    def _indexer(qt, L):
        for c0 in range(0, L, 512):
            cw = min(512, L - c0)
            for h in range(8):
                ip_t, ip_b = ips[state["ip"] % 2]
                state["ip"] += 1
                rows = slice((h % 2) * 64, (h % 2) * 64 + 64)
                op("pe", lambda: T.matmul(out=ip_t[:, :cw], lhsT=iqT[rows, h // 2, qt * 128:(qt + 1) * 128],
                                          rhs=ikT[rows, 0, c0:c0 + cw], start=True, stop=True),
                   r=[iq_b, ik_b], w=[ip_b])
                c_t, c_b = ch_t[state["ch"] % 3]
                state["ch"] += 1
                op("act", lambda: A.activation(out=c_t[:, :cw], in_=ip_t[:, :cw], func=AF.Relu,
                                               scale=wabs[:, qt, h:h + 1]), r=[ip_b, wabs_b], w=[c_b])
                if h == 0:
                    op("dve", lambda: V.tensor_scalar(out=score[:, c0:c0 + cw], in0=c_t[:, :cw],
                                                       scalar1=wsgn[:, qt, 0:1], scalar2=None, op0=ALU.mult),
                       r=[c_b, wsgn_b], w=[score_b])
                else:
                    op("dve", lambda: V.scalar_tensor_tensor(out=score[:, c0:c0 + cw], in0=c_t[:, :cw],
                                                              scalar=wsgn[:, qt, h:h + 1], in1=score[:, c0:c0 + cw],
                                                              op0=ALU.mult, op1=ALU.add),
                       r=[c_b, wsgn_b, score_b], w=[score_b])
        op("dve", lambda: V.tensor_reduce(out=m1[:], in_=score[:, :L], axis=AX.X, op=ALU.max,
                                          apply_absolute_value=True), r=[score_b], w=[m1_b])
        op("dve", lambda: V.tensor_scalar(out=m1[:], in0=m1[:], scalar1=1.0, scalar2=None, op0=ALU.add),
           r=[m1_b], w=[m1_b])
        op("dve", lambda: V.tensor_scalar(out=stp_[:], in0=pw2[:], scalar1=m1[:, 0:1], scalar2=None, op0=ALU.mult),
           r=[pw2_b, m1_b], w=[stp_b])
        op("dve", lambda: V.tensor_tensor(out=score[:, L - 128:L], in0=score[:, L - 128:L], in1=triT[:], op=ALU.add),
           r=[score_b, triT_b], w=[score_b])
        op("dve", lambda: V.memset(mid[:], 0.0), w=[mid_b])
    return prep_qt


def phase_dsa(P, l, C):
    nc, sc = P.nc, P.sc
    op, dma = sc.op, sc.dma
    V, A, G, T = nc.vector, nc.scalar, nc.gpsimd, nc.tensor
    NT, S = P.NT, P.S
    with ExitStack() as st:
        qT, q_b = load_slices(P, st, "ds_q", C["qkT_d"], C["qkT_b"], SL_SQ, 2)
        kT, k_b = load_slices(P, st, "ds_k", C["qkT_d"], C["qkT_b"], SL_SK, 2)
        vT, v_b = load_v(P, st, "ds_v", C["vtm_d"], C["vtm_b"], 776, 260)
        pipe = AttnPipe(P, st, "ds")
        nmask = [P.sb("ds_nm%d" % i, [128, S], BF16, st) for i in range(2)]
        rcp, rcp_b = P.sb("ds_rcp", [128, 2], F32, st)
        ot = [P.sb("ds_ot%d" % i, [128, 256], BF16, st) for i in range(2)]
        state = {"n": 0}

        def prep(job):
            qt, ss = job["qt"], job["ss"]
            if ss == 1:
                job["nm"] = job["prev"]["nm"]
                return
            L = (qt + 1) * 128
            nm_t, nm_b = nmask[state["n"] % 2]
            state["n"] += 1
            job["nm"] = (nm_t, nm_b)
            dma("sp", nm_t[:, :L], C["nmask_d"][qt, :, 0:L], r=[C["nmask_b"][qt]], w=[nm_b])

        def make_mask(job):
            def f(si, kt):
                nm_t, nm_b = job["nm"]
                return (nm_t[:, kt * 128:(kt + 1) * 128], C["ident_h"][:], [nm_b, C["ident_h_b"]])
            return f

        def fin(job, acc_t, acc_b):
            qt, ss = job["qt"], job["ss"]
            o_t, o_b = job["ot"]
            op("dve", lambda: V.reciprocal(out=rcp[:], in_=acc_t[:, 64:130:65]), r=[acc_b], w=[rcp_b])
            for hh in range(2):
                op("dve", lambda: V.tensor_scalar(out=o_t[:, (ss * 2 + hh) * 64:(ss * 2 + hh + 1) * 64],
                                                   in0=acc_t[:, hh * 65:hh * 65 + 64], scalar1=rcp[:, hh:hh + 1],
                                                   scalar2=None, op0=ALU.mult), r=[acc_b, rcp_b], w=[o_b])
            if ss == 1:
                dma("sp", C["mix_d"][qt * 128:(qt + 1) * 128, 768:1024], o_t[:], r=[o_b], w=[C["mix_b"]])

        jobs = []
        prev = None
        for qt in range(NT):
            for ss in range(2):
                job = dict(qt=qt, ss=ss, ktiles=list(range(qt + 1)), prep=prep, fin=fin, ot=ot[qt % 2], prev=prev,
                           streams=[dict(q=(qT, q_b, ss, hh), k=(kT, k_b, ss, hh),
                                         v=(vT, v_b, (ss * 2 + hh) * 65, 65), aoff=hh * 65) for hh in range(2)])
                job["maskfn"] = make_mask(job)
                jobs.append(job)
                prev = job
        pipe.run(jobs)


def phase_ffn(P, l, C, xin_d, xin_b, xout_d, xout_b):
    nc, sc = P.nc, P.sc
    op, dma = sc.op, sc.dma
    V, A, G, T = nc.vector, nc.scalar, nc.gpsimd, nc.tensor
    NT = P.NT
    ext = C["ext"]
    TG = min(2, NT)
    TW = TG * 128
    NF = D_FF // 128
    ident_h, ident_h_b = C["ident_h"], C["ident_h_b"]
    with ExitStack() as st:
        g_f, g_f_b = P.sb("g_f", [128, 8], F32, st)
        gsrc = ext["ffn_norm_g"][0]
        dma("sp", g_f[:], bass.AP(tensor=gsrc.tensor, offset=l * D, ap=[[1, 128], [128, 8]]),
            r=[ext["ffn_norm_g"][1]], w=[g_f_b], allow_slow_non_contiguous=True)
        wo, wo_b = load_weight(P, st, "w_out_b", ext["w_out"][0][l], ext["w_out"][1], D, D, None, None)
        wg, wg_b = load_weight(P, st, "w_gate_b", ext["w_gate"][0][l], ext["w_gate"][1], D, D_FF, g_f, g_f_b)
        wu, wu_b = load_weight(P, st, "w_up_b", ext["w_up"][0][l], ext["w_up"][1], D, D_FF, g_f, g_f_b)
        wd, wd_b = load_weight(P, st, "w_down_b", ext["w_down"][0][l], ext["w_down"][1], D_FF, D, None, None)
        mxs = [P.sb("e_mx%d" % i, [128, D], BF16, st) for i in range(2)]
        xts = [P.sb("e_xt%d" % i, [128, D], F32, st) for i in range(2)]

        def eload(t):
            rows_ = slice(t * 128, (t + 1) * 128)
            dma("sp", mxs[t % 2][0][:], C["mix_d"][rows_, :], r=[C["mix_b"]], w=[mxs[t % 2][1]])
            dma("sp", xts[t % 2][0][:], xin_d[rows_, :], r=[xin_b], w=[xts[t % 2][1]])
        mixT, mixT_b = P.sb("e_mixT", [128, 8, 128], BF16, st)
        x1, x1_b = P.sb("e_x1", [128, TG, D], F32, st)
        junk, junk_b = P.sb("e_junk", [128, D], BF16, st)
        ssum, ssum_b = P.sb("e_ssum", [128, 1], F32, st)
        hb, hb_b = P.sb("e_hb", [128, D], BF16, st)
        h2T, h2T_b = P.sb("e_h2T", [128, 8, TW], BF16, st)
        sg = [P.sb("e_sg%d" % i, [128, TW], F32, st) for i in range(2)]
        aT, aT_b = P.sb("e_aT", [128, NF, TW], BF16, st)
        outt, outt_b = P.sb("e_out", [128, D], F32, st)
        ptr, ptr_b = P.ps("e_ptr", [128, 1024], BF16, st)
        pp = [P.ps("e_pp%d" % i, [128, 512], F32, st) for i in range(2)]
        gp = [P.ps("e_gp%d" % i, [128, 512], F32, st) for i in range(2)]
        up = [P.ps("e_up%d" % i, [128, 512], F32, st) for i in range(2)]
        npp = 0
        for g in range(NT // TG):
            for ti in range(TG):
                t = g * TG + ti
                rows = slice(t * 128, (t + 1) * 128)
                if t == 0:
                    eload(0)
                if t + 1 < NT:
                    eload(t + 1)
                mx, mx_b = mxs[t % 2]
                xt, xt_b = xts[t % 2]
                for kc in range(8):
                    op("pe", lambda: T.transpose(out=ptr[:, kc * 128:(kc + 1) * 128], in_=mx[:, kc * 128:(kc + 1) * 128],
                                                 identity=ident_h[:]), r=[mx_b, ident_h_b], w=[ptr_b], inc=(kc == 7))
                op("act", lambda: A.copy(out=mixT[:].rearrange("p a b -> p (a b)"), in_=ptr[:]), r=[ptr_b], w=[mixT_b])
                for cc in range(2):
                    p_t, p_b = pp[npp % 2]
                    npp += 1
                    for kc in range(8):
                        op("pe", lambda: T.matmul(out=p_t[:], lhsT=mixT[:, kc, :], rhs=wo[:, kc, cc * 512:(cc + 1) * 512],
                                                  start=(kc == 0), stop=(kc == 7)), r=[mixT_b, wo_b], w=[p_b], inc=(kc == 7))
                    op("dve", lambda: V.tensor_tensor(out=x1[:, ti, cc * 512:(cc + 1) * 512], in0=p_t[:],
                                                      in1=xt[:, cc * 512:(cc + 1) * 512], op=ALU.add),
                       r=[p_b, xt_b], w=[x1_b])
                op("act", lambda: A.activation(out=junk[:], in_=x1[:, ti, :], func=AF.Square, accum_out=ssum[:]),
                   r=[x1_b], w=[junk_b, ssum_b])
                op("act", lambda: A.activation(out=ssum[:], in_=ssum[:], func=AF.Sqrt, scale=1.0 / D, bias=EPS),
                   r=[ssum_b], w=[ssum_b])
                op("dve", lambda: V.reciprocal(out=ssum[:], in_=ssum[:]), r=[ssum_b], w=[ssum_b])
                op("dve", lambda: V.tensor_scalar(out=hb[:], in0=x1[:, ti, :], scalar1=ssum[:, 0:1], scalar2=None,
                                                   op0=ALU.mult), r=[x1_b, ssum_b], w=[hb_b])
                for kc in range(8):
                    op("pe", lambda: T.transpose(out=ptr[:, kc * 128:(kc + 1) * 128], in_=hb[:, kc * 128:(kc + 1) * 128],
                                                 identity=ident_h[:]), r=[hb_b, ident_h_b], w=[ptr_b], inc=(kc == 7))
                op("act", lambda: A.copy(out=h2T[:, :, ti * 128:(ti + 1) * 128],
                                         in_=ptr[:].rearrange("p (a b) -> p a b", b=128)), r=[ptr_b], w=[h2T_b])
            for fc in range(NF):
                g_t, g_b = gp[fc % 2]
                u_t, u_b = up[fc % 2]
                for kc in range(8):
                    op("pe", lambda: T.matmul(out=g_t[:, :TW], lhsT=wg[:, kc, fc * 128:(fc + 1) * 128], rhs=h2T[:, kc, :],
                                              start=(kc == 0), stop=(kc == 7)), r=[wg_b, h2T_b], w=[g_b], inc=(kc == 7))
                for kc in range(8):
                    op("pe", lambda: T.matmul(out=u_t[:, :TW], lhsT=wu[:, kc, fc * 128:(fc + 1) * 128], rhs=h2T[:, kc, :],
                                              start=(kc == 0), stop=(kc == 7)), r=[wu_b, h2T_b], w=[u_b], inc=(kc == 7))
                s_t, s_b = sg[fc % 2]
                op("act", lambda: A.activation(out=s_t[:], in_=g_t[:, :TW], func=AF.Silu), r=[g_b], w=[s_b])
                op("dve", lambda: V.tensor_tensor(out=aT[:, fc, :], in0=u_t[:, :TW], in1=s_t[:], op=ALU.mult),
                   r=[u_b, s_b], w=[aT_b])
            for ti in range(TG):
                t = g * TG + ti
                rows = slice(t * 128, (t + 1) * 128)
                for cc in range(2):
                    p_t, p_b = pp[npp % 2]
                    npp += 1
                    for fc in range(NF):
                        op("pe", lambda: T.matmul(out=p_t[:], lhsT=aT[:, fc, ti * 128:(ti + 1) * 128],
                                                  rhs=wd[:, fc, cc * 512:(cc + 1) * 512], start=(fc == 0), stop=(fc == NF - 1)),
                           r=[aT_b, wd_b], w=[p_b], inc=(fc == NF - 1))
                    op("dve", lambda: V.tensor_tensor(out=outt[:, cc * 512:(cc + 1) * 512], in0=p_t[:],
                                                      in1=x1[:, ti, cc * 512:(cc + 1) * 512], op=ALU.add),
                       r=[p_b, x1_b], w=[outt_b])
                dma("sp", xout_d[rows, :], outt[:], r=[outt_b], w=[xout_b])
```
